# Optimizing a Trainium2 kernel written in Bass

```python
import jax
import jax.numpy as jnp
from jax import lax
import numpy as np

D_MODEL = 1024
BATCH = 2
SEQ = 8192
DEPTH = 1

D_MIX = D_MODEL
D_POOL = D_MIX // 2
D_ATTN = D_MIX - D_POOL
POOL_WINDOWS = (2, 4, 8, 16)
N_POOL_GROUPS = len(POOL_WINDOWS)
POOL_GROUP_DIM = D_POOL // N_POOL_GROUPS
HEAD_DIM = 64
N_HEADS = D_ATTN // HEAD_DIM
DILATION_PAIRS = ((128, 1), (512, 4), (2048, 16))
Q_BLOCK = 128
D_PROJ_IN = D_POOL + 3 * D_ATTN
D_FF = 2816
N_MOD = 9
EPS = 1e-6

kernel_name = "hybrid_pool_dilated_attn_macaron_block"


def rmsnorm(x, g):
    xf = x.astype(jnp.float32)
    y = xf * lax.rsqrt(jnp.mean(xf * xf, axis=-1, keepdims=True) + EPS)
    return (y * g.astype(jnp.float32)).astype(x.dtype)


def modulate(n, shift, scale):
    return n * (1 + scale) + shift


def swiglu(n, w_gate, w_up, w_down):
    return (jax.nn.silu(n @ w_gate) * (n @ w_up)) @ w_down


def alibi_slopes(n_heads):
    return jnp.exp2(-8.0 * jnp.arange(1, n_heads + 1, dtype=jnp.float32) / n_heads)


def multiscale_pool(u, w_pool, pool_scale):
    B, S, C = u.shape
    uf = u.astype(jnp.float32)
    cs0 = jnp.pad(jnp.cumsum(uf, axis=1), ((0, 0), (1, 0), (0, 0)))
    t = jnp.arange(S)
    groups = []
    for g, w in enumerate(POOL_WINDOWS):
        sl = slice(g * POOL_GROUP_DIM, (g + 1) * POOL_GROUP_DIM)
        csg = cs0[..., sl]
        lower = jnp.pad(csg[:, :S + 1 - w], ((0, 0), (w - 1, 0), (0, 0)))
        count = jnp.minimum(t + 1, w).astype(jnp.float32)[None, :, None]
        mean = (csg[:, 1:] - lower) / count
        groups.append(mean - uf[..., sl])
    pooled = jnp.stack(groups, axis=2)
    y = jnp.einsum('bsgc,gcd->bsgd', pooled, w_pool.astype(jnp.float32)).reshape(B, S, C)
    return (y * pool_scale.astype(jnp.float32)).astype(u.dtype)


def dilated_attention(q, k, v):
    B, S, H, Dh = q.shape
    n_blk = S // Q_BLOCK
    slopes = alibi_slopes(H)
    scale = Dh ** -0.5

    def block(i):
        t0 = i * Q_BLOCK
        t = t0 + jnp.arange(Q_BLOCK)
        qb = lax.dynamic_slice_in_dim(q, t0, Q_BLOCK, axis=1).astype(jnp.float32) * scale
        mxs, dens, nums = [], [], []
        for window, dil in DILATION_PAIRS:
            dist = dil * jnp.arange(window // dil + 1)
            idx = t[:, None] - dist[None, :]
            valid = idx >= 0
            idx = jnp.maximum(idx, 0)
            kg = jnp.take(k, idx, axis=1).astype(jnp.float32)
            vg = jnp.take(v, idx, axis=1).astype(jnp.float32)
            s = jnp.einsum('bqhd,bqjhd->bqhj', qb, kg)
            s = s - slopes[:, None] * dist.astype(jnp.float32)[None, :]
            s = jnp.where(valid[None, :, None, :], s, -jnp.inf)
            mx = jnp.max(s, axis=-1)
            p = jnp.exp(s - mx[..., None])
            dens.append(jnp.sum(p, axis=-1))
            nums.append(jnp.einsum('bqhj,bqjhd->bqhd', p, vg))
            mxs.append(mx)
        m_all = jnp.stack(mxs)
        w_r = jnp.exp(m_all - jnp.max(m_all, axis=0))
        num = sum(w_r[r][..., None] * nums[r] for r in range(len(DILATION_PAIRS)))
        den = sum(w_r[r] * dens[r] for r in range(len(DILATION_PAIRS)))
        return (num / den[..., None]).astype(q.dtype)

    out = lax.map(block, jnp.arange(n_blk))
    return jnp.moveaxis(out, 0, 1).reshape(B, S, H, Dh)


def hybrid_mixer(n, w_in, w_pool, pool_scale, w_out):
    B, S, _ = n.shape
    z = n @ w_in
    u, q, k, v = jnp.split(z, [D_POOL, D_POOL + D_ATTN, D_POOL + 2 * D_ATTN], axis=-1)
    y_pool = multiscale_pool(u, w_pool, pool_scale)
    hs = (B, S, N_HEADS, HEAD_DIM)
    y_attn = dilated_attention(q.reshape(hs), k.reshape(hs), v.reshape(hs)).reshape(B, S, D_ATTN)
    return jnp.concatenate([y_pool, y_attn], axis=-1) @ w_out


def setup_inputs(seed: int = 0) -> dict:
    key = jax.random.key(seed)
    ks = jax.random.split(key, 20)
    f32 = jnp.float32
    L, D = DEPTH, D_MODEL

    def nrm(k, shape, fan_in, mult=1.0):
        return jax.random.normal(k, shape, f32) * (mult * fan_in ** -0.5)

    def gain(k, shape):
        return 1.0 + 0.05 * jax.random.normal(k, shape, f32)

    return {
        "x": jax.random.normal(ks[0], (BATCH, SEQ, D), f32),
        "c": jax.random.normal(ks[1], (BATCH, D), f32),
        "w_ada": nrm(ks[2], (L, D, N_MOD * D), D, 0.5),
        "b_ada": 0.02 * jax.random.normal(ks[3], (L, N_MOD * D), f32),
        "g_ffn1": gain(ks[4], (L, D)),
        "w1_gate": nrm(ks[5], (L, D, D_FF), D),
        "w1_up": nrm(ks[6], (L, D, D_FF), D),
        "w1_down": nrm(ks[7], (L, D_FF, D), D_FF),
        "g_mix": gain(ks[8], (L, D)),
        "w_in": nrm(ks[9], (L, D, D_PROJ_IN), D),
        "w_pool": nrm(ks[10], (L, N_POOL_GROUPS, POOL_GROUP_DIM, POOL_GROUP_DIM), POOL_GROUP_DIM),
        "pool_scale": gain(ks[11], (L, D_POOL)),
        "w_out": nrm(ks[12], (L, D_MIX, D), D_MIX),
        "g_ffn2": gain(ks[13], (L, D)),
        "w2_gate": nrm(ks[14], (L, D, D_FF), D),
        "w2_up": nrm(ks[15], (L, D, D_FF), D),
        "w2_down": nrm(ks[16], (L, D_FF, D), D_FF),
        "g_final": gain(ks[17], (D,)),
    }


def reference(x, c, w_ada, b_ada, g_ffn1, w1_gate, w1_up, w1_down, g_mix, w_in, w_pool,
              pool_scale, w_out, g_ffn2, w2_gate, w2_up, w2_down, g_final):
    h = x
    for l in range(DEPTH):
        mod = (jax.nn.silu(c) @ w_ada[l] + b_ada[l])[:, None, :]
        sh1, sc1, gt1, sh2, sc2, gt2, sh3, sc3, gt3 = jnp.split(mod, N_MOD, axis=-1)
        n = modulate(rmsnorm(h, g_ffn1[l]), sh1, sc1)
        h = h + 0.5 * gt1 * swiglu(n, w1_gate[l], w1_up[l], w1_down[l])
        n = modulate(rmsnorm(h, g_mix[l]), sh2, sc2)
        h = h + gt2 * hybrid_mixer(n, w_in[l], w_pool[l], pool_scale[l], w_out[l])
        n = modulate(rmsnorm(h, g_ffn2[l]), sh3, sc3)
        h = h + 0.5 * gt3 * swiglu(n, w2_gate[l], w2_up[l], w2_down[l])
    return rmsnorm(h, g_final)
```

```python
import numpy as np
import concourse.bass as bass
import concourse.mybir as mybir
from concourse.bass_utils import run_bass_kernel_spmd

F32 = mybir.dt.float32
BF16 = mybir.dt.bfloat16
AF = mybir.ActivationFunctionType
ALU = mybir.AluOpType

D = 1024
DFF = 2816
NCORES = 8
OWN = 2048
TOK = 4096
ST = 1024
NGRP = 11
EPS = 1e-6
STAGES = ("ffn1", "mix", "ffn2")


class Sched:
    CE = ('pe', 'act', 'dve', 'pool', 'sp')

    def __init__(self):
        self.ops = []
        self.lastw = {}
        self.readers = {}
        self.region = {}
        self.regmembers = {}
        self.total_keys = set()

    def set_region(self, res, region, setname):
        self.region[res] = (region, setname)
        self.regmembers.setdefault(region, []).append(res)

    def _conflicts(self, w):
        if w not in self.region:
            return ()
        reg, sn = self.region[w]
        return [r for r in self.regmembers[reg] if self.region[r][1] != sn]

    def op(self, eng, fn, reads=(), writes=(), dkey=None, after=(), inc=16):
        deps = set(after)
        for r in reads:
            if r in self.lastw:
                deps.add(self.lastw[r])
        for w in writes:
            for x in [w] + list(self._conflicts(w)):
                if x in self.lastw:
                    deps.add(self.lastw[x])
                deps.update(self.readers.get(x, {}).values())
        i = len(self.ops)
        deps.discard(i)
        self.ops.append(dict(eng=eng, fn=fn, deps=deps, dkey=dkey, inc=inc))
        for r in reads:
            d = self.readers.setdefault(r, {})
            d[('dma', i) if dkey is not None else eng] = i
        for w in writes:
            self.lastw[w] = i
            self.readers[w] = {}
        return i

    def emit(self, nc, final_wait_keys=()):
        ops = self.ops

        def fdeps(o):
            out = []
            for d in o['deps']:
                x = ops[d]
                if x['dkey'] is None and x['eng'] == 'pe' and o['eng'] == 'pe' and o['dkey'] is None:
                    continue
                if x['dkey'] is not None and x['dkey'] in self.total_keys and x['dkey'] == o['dkey']:
                    continue
                out.append(d)
            return out
        need = set()
        for o in ops:
            o['fd'] = fdeps(o)
            need.update(o['fd'])
        cnt = {}
        dcnt = {}
        for i, o in enumerate(ops):
            if o['dkey'] is not None:
                k = o['dkey']
                dcnt[k] = dcnt.get(k, 0) + o['inc']
                o['sig'] = (('d', k), dcnt[k])
            elif i in need:
                e = o['eng']
                cnt[e] = cnt.get(e, 0) + 1
                o['sig'] = (('e', e), cnt[e])
            else:
                o['sig'] = None
        for o in ops:
            if o['dkey'] in self.total_keys:
                o['sig'] = (('d', o['dkey']), dcnt[o['dkey']])
        semnames = sorted(set(o['sig'][0] for o in ops if o['sig'] is not None), key=str)
        with nc.cleanup_on_exit():
            self._emit_body(nc, ops, semnames, per_eng_init=True, final_wait_keys=final_wait_keys, dcnt=dcnt)

    def _emit_body(self, nc, ops, semnames, per_eng_init, final_wait_keys, dcnt):
        sems = {}
        for n, sn in enumerate(semnames):
            sems[sn] = nc.alloc_semaphore("s%d" % n)
        per_eng = {e: [] for e in self.CE}
        for i, o in enumerate(ops):
            per_eng[o['eng']].append(i)
        finals = [(('d', k), dcnt[k]) for k in final_wait_keys if k in dcnt]

        def run_engine(e, h):
            waited = {}
            for i in per_eng[e]:
                o = ops[i]
                reqs = {}
                for d in o['fd']:
                    sn, v = ops[d]['sig']
                    if reqs.get(sn, 0) < v:
                        reqs[sn] = v
                for sn, v in reqs.items():
                    if waited.get(sn, 0) >= v:
                        continue
                    h.wait_ge(sems[sn], v)
                    waited[sn] = v
                ins = o['fn'](h)
                if o['sig'] is not None:
                    ins.then_inc(sems[o['sig'][0]], o['inc'] if o['dkey'] is not None else 1)
            if e == 'sp':
                for sn, v in finals:
                    h.wait_ge(sems[sn], v)
        with nc.Block() as block:
            @block.tensor
            def _(h):
                run_engine('pe', h)

            @block.scalar
            def _(h):
                run_engine('act', h)

            @block.vector
            def _(h):
                run_engine('dve', h)

            @block.gpsimd
            def _(h):
                run_engine('pool', h)

            @block.sync
            def _(h):
                run_engine('sp', h)


def weight_sequence():
    seq = []
    for s in range(2):
        seq += [('ffn', 1, gi) for gi in range(NGRP)]
        seq += [('win', 2), ('win', 3), ('win', 0)]
    for s in range(2):
        seq += [('win', 0), ('win', 1), ('wout', 0), ('wout', 1)]
        seq += [('ffn', 2, gi) for gi in range(NGRP)]
    return seq


def build_program():
    nc = bass.Bass("TRN2", target_bir_lowering=False)
    S = Sched()

    def din(name, shape):
        return nc.dram_tensor(name, list(shape), F32, kind="ExternalInput").ap()
    x4 = din("x4", [OWN, D])
    idx_d = nc.dram_tensor("idx", [128, 5], mybir.dt.int32, kind="ExternalInput").ap()
    cT_d = din("cT", [128, 8])
    flag_d = din("flag", [128, 1])
    pcfix_d = din("pcfix", [128, 64])
    w_ada = din("w_ada", [D, 9 * D])
    bada_d = din("bada", [1, 9 * D])
    gvec_d = din("gvec", [128, 36])
    wg_d = {1: din("w1_gate", [D, DFF]), 2: din("w2_gate", [D, DFF])}
    wu_d = {1: din("w1_up", [D, DFF]), 2: din("w2_up", [D, DFF])}
    wd_d = {1: din("w1_down", [DFF, D]), 2: din("w2_down", [DFF, D])}
    w_in = din("w_in", [D, 2 * D])
    w_pool = din("w_pool", [4, 128, 128])
    w_out = din("w_out", [D, D])
    ident_d = din("ident", [128, 128])
    dm_d = din("dmask", [128, 12 * 256])
    out_d = nc.dram_tensor("out", [OWN, D], F32, kind="ExternalOutput").ap()

    h1buf = [nc.dram_tensor("h1buf%d" % s_, [128, 8 * ST], F32) for s_ in range(2)]
    n2buf = [nc.dram_tensor("n2buf%d" % s_, [128, 8 * ST], BF16) for s_ in range(2)]
    sendK = [nc.dram_tensor("sendK%d" % s_, [128, 4 * ST], BF16) for s_ in range(2)]
    sendV = [nc.dram_tensor("sendV%d" % s_, [128, 4 * ST], BF16) for s_ in range(2)]
    gathK = [nc.dram_tensor("gathK%d" % s_, [512, 4 * ST], BF16) for s_ in range(2)]
    gathV = [nc.dram_tensor("gathV%d" % s_, [512, 4 * ST], BF16) for s_ in range(2)]
    sendU = nc.dram_tensor("sendU", [128, 64], F32)
    stashU = nc.dram_tensor("stashU", [128, 64], F32)
    gathU = nc.dram_tensor("gathU", [512, 64], F32)
    RG = [[0, 1, 2, 3], [4, 5, 6, 7]]

    cur = [16512]

    def alloc(name, shape, dt, off=None):
        nbytes = int(np.prod(shape[1:])) * (4 if dt == F32 else 2)
        if off is None:
            off = cur[0]
            cur[0] += (nbytes + 31) // 32 * 32
        return nc.alloc_sbuf_tensor_at(name, list(shape), dt, offset=off)
    KT = alloc("KT", [128, 4, TOK], BF16)
    VT = alloc("VT", [128, 4, TOK], BF16)
    DM = alloc("DM", [128, 12, 256], BF16)
    wpool = alloc("wpool", [128, 4, 128], BF16)
    VtAll = alloc("VtAll", [128, 9, 256], BF16)
    identF = alloc("identF", [128, 128], F32)
    identB = alloc("identB", [128, 128], BF16)
    onesB = alloc("onesB", [128, 128], BF16)
    modT = alloc("modT", [128, 72], F32)
    gvec = alloc("gvec_sb", [128, 36], F32)
    G1 = alloc("G1", [128, 8], F32)
    GT1 = alloc("GT1", [128, 8], F32)
    G2 = alloc("G2", [128, 8], F32)
    G3 = alloc("G3", [128, 8], F32)
    GT3 = alloc("GT3", [128, 8], F32)
    flag = alloc("flag_sb", [128, 1], F32)
    idx_sb = alloc("idx_sb", [128, 5], mybir.dt.int32)
    onesv = alloc("onesv", [128, 1], F32)
    epsv = alloc("epsv", [128, 1], F32)
    csb = alloc("csb", [128, 8], F32)
    silc = alloc("silc", [128, 8], BF16)
    pcfix = alloc("pcfix_sb", [128, 4, 16], F32)
    carry = alloc("carry", [128, 4, 16], F32)
    one11 = alloc("one11", [1, 1], F32)
    hT = alloc("hT", [128, 8, ST], F32)
    Wbase = cur[0]
    cur[0] += 2 * 12288
    wgu = [alloc("wgu%d" % s, [128, 8, 512], BF16, off=Wbase + s * 12288) for s in range(2)]
    wdn = [alloc("wdn%d" % s, [128, 2, 1024], BF16, off=Wbase + s * 12288 + 8192) for s in range(2)]
    wav = [alloc("wav%d" % s, [128, 8, 256], F32, off=Wbase + s * 12288) for s in range(2)]
    wov = [alloc("wov%d" % s, [128, 4, 1024], BF16, off=Wbase + s * 12288) for s in range(2)]
    R0 = cur[0]
    nT = alloc("nT", [128, 8, ST], BF16, off=R0)
    yT = alloc("yT", [128, 8, ST], BF16, off=R0 + 16384)
    QTm = alloc("QTm", [128, 2, 4, ST], BF16, off=R0 + 32768)
    o = R0 + 49152
    sq = [alloc("sq%d" % i, [128, 512], BF16, off=o + 1024 * i) for i in range(2)]
    rstd = alloc("rstd", [128, 512], F32, off=o + 2048)
    std = rstd
    tmp = [alloc("tmp%d" % i, [128, 512], F32, off=o + 4096 + 2048 * i) for i in range(2)]
    o = R0 + 57344
    actT = [[alloc("actT%d%d" % (a, j), [128, 512], BF16, off=o + 2048 * a + 1024 * j) for j in range(2)]
            for a in range(2)]
    sg = [alloc("sg%d" % j, [128, 512], BF16, off=o + 4096 + 1024 * j) for j in range(2)]
    PT = [alloc("PT%d" % i, [128, 512], BF16, off=o + 1024 * i) for i in range(4)]
    rec = alloc("rec", [128, 512], F32, off=o + 4096)
    o = R0 + 63488
    ub = [alloc("ub%d" % i, [128, 528], F32, off=o + 2112 * i) for i in range(2)]
    pw = [alloc("pw%d" % i, [128, 528], F32, off=o + 4224 + 2112 * i) for i in range(2)]
    pooled = [alloc("pooled%d" % i, [128, 512], BF16, off=o + 8448 + 1024 * i) for i in range(2)]
    Oacc = alloc("Oacc", [128, 2, ST], F32, off=o)
    ot = alloc("ot", [128, 8, 128], F32, off=o)
    ostage = [alloc("ostage%d" % i, [128, D], F32, off=o + 4096 + 4096 * i) for i in range(2)]
    xs = [alloc("xs%d" % i, [128, D], F32, off=o + 4096 * i) for i in range(2)]
    modrow = [alloc("modrow%d" % i, [1, 256], F32, off=R0 + 32768 + 1024 * i) for i in range(2)]
    brow = [alloc("brow%d" % i, [1, 256], F32, off=R0 + 32768 + 2048 + 1024 * i) for i in range(4)]
    wa2 = [alloc("wa2_%d" % i, [128, 8, 256], BF16, off=R0 + 16384 + 4096 * i) for i in range(4)]
    assert o + 12288 <= 229344, (o + 12288)
    print("SBUF slack bytes:", 229344 - (o + 12288))

    for a_ in range(2):
        for j in range(2):
            S.set_region(('actT', a_, j), 'R2', 'ffn')
    for j in range(2):
        S.set_region(('sg', j), 'R2', 'ffn')
    for i in range(4):
        S.set_region(('PT', i), 'R2', 'attn')
    S.set_region('rec', 'R2', 'attn')
    for i in range(2):
        S.set_region(('ub', i), 'R3', 'pool')
        S.set_region(('pw', i), 'R3', 'pool')
        S.set_region(('pooled', i), 'R3', 'pool')
        S.set_region(('ostage', i), 'R3', 'fin')
        S.set_region(('xs', i), 'R3', 'xs')
        S.set_region(('modrow', i), 'RQ', 'mod')
    for i in range(4):
        S.set_region(('brow', i), 'RQ', 'mod')
        S.set_region(('ada', i), 'RY', 'ada')
    for g_ in range(4):
        S.set_region(('QT', g_), 'RQ', 'qt')
    for c_ in range(8):
        S.set_region(('yT', c_), 'RY', 'yt')
    for hh in range(2):
        S.set_region(('Oacc', hh), 'R3', 'attn')
    for c in range(8):
        S.set_region(('ot', c), 'R3', 'fin')

    banks = [nc.alloc_psum_tensor("bk%d" % i, [128, 512], F32) for i in range(8)]
    banksB = [b[:, :].bitcast(BF16) for b in banks]

    def bk(i):
        return ('bk', i)

    wada_v = w_ada.rearrange("(kc p) n -> p kc n", p=128)

    S.total_keys.add('const')
    S.total_keys.add('constc')

    def cload(dst, src, res):
        S.op('sp', lambda h: h.dma_start(out=dst, in_=src), writes=[res], dkey='const')
    cload(identF[:], ident_d[:], 'identF')
    cload(csb[:], cT_d[:], 'csb')
    cload(gvec[:], gvec_d[:], 'gvec')
    cload(flag[:], flag_d[:], 'flag')
    cload(idx_sb[:], idx_d[:], 'idx')
    cload(pcfix[:].rearrange("p a b -> p (a b)"), pcfix_d[:], 'pcfix')
    for a0 in range(12):
        S.op('pool', lambda h, a0=a0: h.dma_start(out=DM[:, a0, :], in_=dm_d[:, a0 * 256:(a0 + 1) * 256]),
             writes=[('DM', a0)], dkey='constc')
    S.op('pool', lambda h: h.dma_start(out=wpool[:], in_=w_pool.rearrange("g c d -> c g d")),
         writes=['wpool'], dkey='constc')
    S.op('pool', lambda h: h.dma_start(out=identB[:], in_=ident_d[:]), writes=['identB'], dkey='constc')
    S.op('dve', lambda h: h.memset(onesB[:], 1.0), writes=['onesB'])
    S.op('dve', lambda h: h.memset(onesv[:], 1.0), writes=['onesv'])
    S.op('dve', lambda h: h.memset(epsv[:], EPS), writes=['epsv'])
    S.op('dve', lambda h: h.memset(one11[:], 1.0), writes=['one11'])
    for i in range(3, 9):
        S.op('dve', lambda h, i=i: h.memset(VtAll[:, i, :], 1.0), writes=[('Vt', i)])
    for i in range(3):
        for hf in range(2):
            S.op('dve', lambda h, i=i, hf=hf: h.tensor_scalar(
                out=VtAll[:, i, hf * 128:(hf + 1) * 128], in0=onesB[:], scalar1=flag[:, 0:1], scalar2=None,
                op0=ALU.mult), reads=['onesB', 'flag'], writes=[('Vt', i)])
    def kv_zero_part(g_):
        S.op('dve', lambda h: h.memset(KT[:, g_, 3072:4096], 0.0), writes=[('KT', 3, g_)])
        S.op('dve', lambda h: h.memset(VT[:, g_, 3072:4096], 0.0), writes=[('VT', 3, g_)])
        S.op('dve', lambda h: h.memset(KT[:, g_, 0:1024], 0.0), writes=[('KT', 0, g_)])
        S.op('dve', lambda h: h.memset(VT[:, g_, 0:1024], 0.0), writes=[('VT', 0, g_)])
    def qtm_zero():
        S.op('dve', lambda h: h.memset(QTm[64:128, 0, :, :], 0.0), writes=[('QT', g) for g in range(4)])
        S.op('dve', lambda h: h.memset(QTm[0:64, 1, :, :], 0.0), writes=[('QT', g) for g in range(4)])

    wseq = weight_sequence()
    W = dict(issued=0, acq=0)
    win_v = w_in.rearrange("(kc p) n -> p kc n", p=128)
    wg_v = {k: v.rearrange("(kc p) n -> p kc n", p=128) for k, v in wg_d.items()}
    wu_v = {k: v.rearrange("(kc p) n -> p kc n", p=128) for k, v in wu_d.items()}
    WALL = lambda s: [('W', s, 'g'), ('W', s, 'u'), ('W', s, 'd')]

    def w_issue(k):
        desc = wseq[k]
        s = k % 2
        if False:
            pass
        elif desc[0] == 'ffn':
            which, gi = desc[1], desc[2]
            f0 = gi * 256
            S.op('pool', lambda h: h.dma_start(out=wgu[s][:, :, 0:256], in_=wg_v[which][:, :, f0:f0 + 256]),
                 writes=[('W', s, 'g')], dkey=('W', s, 'g'))
            S.op('pool', lambda h: h.dma_start(out=wgu[s][:, :, 256:512], in_=wu_v[which][:, :, f0:f0 + 256]),
                 writes=[('W', s, 'u')], dkey=('W', s, 'u'))
            S.op('pool', lambda h: h.dma_start(
                out=wdn[s][:], in_=wd_d[which][f0:f0 + 256, :].rearrange("(j p) n -> p j n", p=128)),
                writes=[('W', s, 'd')], dkey=('W', s, 'd'))
        elif desc[0] == 'win':
            q = desc[1]
            S.op('pool', lambda h: h.dma_start(out=wgu[s][:], in_=win_v[:, :, q * 512:(q + 1) * 512]),
                 writes=[('W', s, 'g'), ('W', s, 'u')], dkey=('W', s, 'g'))
        elif desc[0] == 'wout':
            hf = desc[1]
            S.op('pool', lambda h: h.dma_start(
                out=wov[s][:], in_=w_out[hf * 512:(hf + 1) * 512, :].rearrange("(kc p) n -> p kc n", p=128)),
                writes=[('W', s, 'g'), ('W', s, 'u')], dkey=('W', s, 'g'))
        else:
            raise ValueError(desc)

    def w_acquire(desc):
        k = W['acq']
        assert wseq[k] == desc, (k, wseq[k], desc)
        if W['issued'] <= k:
            w_issue(k)
            W['issued'] = k + 1
        W['acq'] = k + 1
        return k % 2

    def w_prefetch(ahead=1):
        k = W['issued']
        if k < len(wseq) and k <= W['acq'] + ahead - 1:
            w_issue(k)
            W['issued'] = k + 1

    S.op('act', lambda h: h.activation(out=silc[:], in_=csb[:], func=AF.Silu), reads=['csb'], writes=['silc'])
    modst = dict(next=0, pend=None)

    def mres(s_):
        return ('modT', s_)

    def gmake(dst, sec, g_off, res):
        S.op('dve', lambda h: h.scalar_tensor_tensor(out=dst[:], in0=modT[:, 8 * sec:8 * sec + 8], scalar=1.0,
                                                     in1=gvec[:, g_off:g_off + 8], op0=ALU.add, op1=ALU.mult),
             reads=[mres(sec), 'gvec'], writes=[res])

    def mod_cols(b):
        for jj in range(2):
            col = 256 + (2 * b + jj) % 8
            S.op('pe', lambda h, b=b, jj=jj, col=col: h.matmul(
                banks[7][:, col:col + 1], lhsT=modrow[b % 2][0:1, jj * 128:(jj + 1) * 128],
                rhs=one11[0:1, 0:1], start=True, stop=True),
                reads=[('modrow', b % 2), 'one11'], writes=[bk(7)])
        if b % 4 == 3:
            sec = b // 4
            S.op('dve', lambda h, sec=sec: h.tensor_copy(out=modT[:, 8 * sec:8 * sec + 8], in_=banks[7][:, 256:264]),
                 reads=[bk(7)], writes=[mres(sec)])
            if sec == 1:
                gmake(G1, 1, 0, 'G1')
            elif sec == 2:
                S.op('dve', lambda h: h.tensor_scalar(out=GT1[:], in0=modT[:, 16:24], scalar1=0.5, scalar2=None,
                                                      op0=ALU.mult), reads=[mres(2)], writes=['GT1'])
            elif sec == 4:
                gmake(G2, 4, 8, 'G2')
            elif sec == 7:
                gmake(G3, 7, 16, 'G3')
            elif sec == 8:
                S.op('dve', lambda h: h.tensor_scalar(out=GT3[:], in0=modT[:, 64:72], scalar1=0.5, scalar2=None,
                                                      op0=ALU.mult), reads=[mres(8)], writes=['GT3'])

    def mod_issue(b):
        if b >= 36 or b < modst.get('issued', 0):
            return
        assert b == modst.get('issued', 0)
        modst['issued'] = b + 1
        k = b % 4
        S.op('pool', lambda h, b=b, k=k: h.dma_start(out=wa2[k][:], in_=wada_v[:, :, b * 256:(b + 1) * 256]),
             writes=[('ada', k)], dkey=('ada', k))
        S.op('sp', lambda h, b=b, k=k: h.dma_start(out=brow[k][:], in_=bada_d[0:1, b * 256:(b + 1) * 256]),
             writes=[('brow', k)], dkey=('brow', k))

    def mod_blocks(n):
        for _ in range(n):
            b = modst['next']
            if b >= 36:
                break
            modst['next'] = b + 1
            k = b % 2
            ka = b % 4
            for b2_ in range(b, b + 4):
                mod_issue(b2_)
            rb = 6 + k
            for kc in range(8):
                S.op('pe', lambda h, kc=kc, ka=ka, rb=rb: h.matmul(
                    banks[rb][0:1, 0:256], lhsT=silc[:, kc:kc + 1], rhs=wa2[ka][:, kc, :],
                    start=(kc == 0), stop=(kc == 7)), reads=['silc', ('ada', ka)], writes=[bk(rb)])
            S.op('dve', lambda h, k=k, ka=ka, rb=rb: h.tensor_tensor(out=modrow[k][:], in0=banks[rb][0:1, 0:256],
                                                                     in1=brow[ka][:], op=ALU.add),
                 reads=[bk(rb), ('brow', ka)], writes=[('modrow', k)])
            if modst['pend'] is not None:
                mod_cols(modst['pend'])
            modst['pend'] = b
        if modst['next'] >= 36 and modst['pend'] is not None:
            mod_cols(modst['pend'])
            modst['pend'] = None
        for b2_ in range(modst['next'], modst['next'] + 3):
            mod_issue(b2_)

    def mod_flush():
        if modst['pend'] is not None:
            mod_cols(modst['pend'])
            modst['pend'] = None
    MODUP = dict(done=False)

    def mod_upfront():
        if not MODUP['done']:
            MODUP['done'] = True
            mod_blocks(8)
            mod_flush()
    SH1 = modT[:, 0:8]
    SH2 = modT[:, 24:32]
    SH3 = modT[:, 48:56]
    GT2 = modT[:, 40:48]
    GFIN = gvec[:, 24:32]
    PSC = gvec[:, 32:36]

    def hres(tt, c):
        return ('hT', tt, c)

    def load_x(st, after_t0=None):
        Tb = (st - 2) * ST
        for i in range(8):
            k = i % 2
            S.op('sp', lambda h, i=i, k=k: h.dma_start(out=xs[k][:], in_=x4[Tb + 128 * i:Tb + 128 * (i + 1), :]),
                 writes=[('xs', k)], dkey=('xs', k))
            tt = i // 4
            for half in range(2):
                b = 6 + half
                for cc in range(4):
                    c = 4 * half + cc
                    S.op('pe', lambda h, k=k, c=c, cc=cc, b=b: h.transpose(
                        out=banks[b][:, cc * 128:(cc + 1) * 128], in_=xs[k][:, c * 128:(c + 1) * 128],
                        identity=identF[:]), reads=[('xs', k), 'identF'], writes=[bk(b)])
                dst = hT[:, 4 * half:4 * half + 4, i * 128:(i + 1) * 128]
                src = banks[b][:, :].rearrange("p (a b) -> p a b", a=4)
                wr = [hres(tt, 4 * half + cc) for cc in range(4)]
                if half == 0:
                    S.op('act', lambda h, dst=dst, src=src: h.activation(out=dst, in_=src, func=AF.Copy),
                         reads=[bk(b)], writes=wr)
                else:
                    S.op('dve', lambda h, dst=dst, src=src: h.tensor_copy(out=dst, in_=src),
                         reads=[bk(b)], writes=wr)
            if i == 3 and after_t0 is not None:
                after_t0()

    def norm_stats(tt):
        cols = slice(tt * 512, (tt + 1) * 512)
        for c in range(8):
            S.op('act', lambda h, c=c: h.activation(out=sq[c % 2][:], in_=hT[:, c, cols], func=AF.Square),
                 reads=[hres(tt, c)], writes=[('sq', c % 2)])
            S.op('pe', lambda h, c=c: h.matmul(banks[6][:, :], lhsT=onesB[:], rhs=sq[c % 2][:],
                                               start=(c == 0), stop=(c == 7)),
                 reads=[('sq', c % 2), 'onesB'], writes=[bk(6)])
        S.op('act', lambda h: h.activation(out=std[:], in_=banks[6][:, :], func=AF.Ln, scale=1.0 / D, bias=epsv[:, 0:1]),
             reads=[bk(6), 'epsv'], writes=['rstd'])
        S.op('act', lambda h: h.activation(out=rstd[:], in_=std[:], func=AF.Exp, scale=-0.5),
             reads=['rstd'], writes=['rstd'])

    def norm_to_nT(tt, Gv, SHv, gres, shres):
        norm_stats(tt)
        cols = slice(tt * 512, (tt + 1) * 512)
        for c in range(8):
            S.op('dve', lambda h, c=c: h.tensor_tensor(out=tmp[c % 2][:], in0=hT[:, c, cols], in1=rstd[:],
                                                       op=ALU.mult),
                 reads=[hres(tt, c), 'rstd'], writes=[('tmp', c % 2)])
            S.op('act', lambda h, c=c: h.activation(out=nT[:, c, cols], in_=tmp[c % 2][:], func=AF.Identity,
                                                    scale=Gv[:, c:c + 1], bias=SHv[:, c:c + 1]),
                 reads=[('tmp', c % 2), gres, shres], writes=[('nT', tt)])

    def ffn(which, GTv, gtres, after_t0=None, iter_hook=None):
        iters = [(gi, tt) for gi in range(NGRP - 2) for tt in range(2)]
        iters += [(NGRP - 2, 0), (NGRP - 1, 0), (NGRP - 2, 1), (NGRP - 1, 1)]
        prefetch_pts = set((gi, 0) for gi in range(NGRP - 1)) | {(NGRP - 2, 1), (NGRP - 1, 1)}
        slots = {}
        prev = [None]

        def down_piece(p):
            if prev[0] is None:
                return
            s, tt, cols, aset = prev[0]
            for c in (2 * p, 2 * p + 1):
                b = 4 + (c % 2)
                for j in range(2):
                    S.op('pe', lambda h, c=c, j=j, b=b, s=s, aset=aset: h.matmul(
                        banks[b][:, :], lhsT=wdn[s][:, j, c * 128:(c + 1) * 128], rhs=actT[aset][j][:],
                        start=(j == 0), stop=(j == 1)),
                        reads=[('W', s, 'd'), ('actT', aset, j)], writes=[bk(b)])
                S.op('dve', lambda h, c=c, b=b, cols=cols: h.scalar_tensor_tensor(
                    out=hT[:, c, cols], in0=banks[b][:, :], scalar=GTv[:, c:c + 1], in1=hT[:, c, cols],
                    op0=ALU.mult, op1=ALU.add),
                    reads=[bk(b), gtres, hres(tt, c)], writes=[hres(tt, c)])

        for it, (gi, tt) in enumerate(iters):
            if iter_hook is not None:
                iter_hook(it)
            if gi not in slots:
                slots[gi] = w_acquire(('ffn', which, gi))
            s = slots[gi]
            cols = slice(tt * 512, (tt + 1) * 512)
            aset = it % 2
            for j in range(2):
                for kc in range(8):
                    S.op('pe', lambda h, kc=kc, j=j, s=s, cols=cols: h.matmul(
                        banks[j][:, :], lhsT=wgu[s][:, kc, j * 128:(j + 1) * 128], rhs=nT[:, kc, cols],
                        start=(kc == 0), stop=(kc == 7)),
                        reads=[('W', s, 'g'), ('nT', tt)], writes=[bk(j)])
                down_piece(2 * j)
                for kc in range(8):
                    S.op('pe', lambda h, kc=kc, j=j, s=s, cols=cols: h.matmul(
                        banks[2 + j][:, :], lhsT=wgu[s][:, kc, 256 + j * 128:256 + (j + 1) * 128],
                        rhs=nT[:, kc, cols], start=(kc == 0), stop=(kc == 7)),
                        reads=[('W', s, 'u'), ('nT', tt)], writes=[bk(2 + j)])
                S.op('act', lambda h, j=j: h.activation(out=sg[j][:], in_=banks[j][:, :], func=AF.Silu),
                     reads=[bk(j)], writes=[('sg', j)])
                S.op('dve', lambda h, j=j, aset=aset: h.tensor_tensor(
                    out=actT[aset][j][:], in0=sg[j][:], in1=banks[2 + j][:, :], op=ALU.mult),
                    reads=[('sg', j), bk(2 + j)], writes=[('actT', aset, j)])
                down_piece(2 * j + 1)
            prev[0] = (s, tt, cols, aset)
            if (gi, tt) in prefetch_pts:
                w_prefetch()
            if (gi, tt) == (NGRP - 2, 1) and after_t0 is not None:
                after_t0()
        for p in range(4):
            down_piece(p)
        prev[0] = None
        w_prefetch()

    def proj_chunk(s, gq, tt, ncols=512, c0=0):
        b = proj_chunk.rot % 2 + 6
        proj_chunk.rot += 1
        cols = slice(tt * 512 + c0, tt * 512 + c0 + ncols)
        for kc in range(8):
            S.op('pe', lambda h, kc=kc, b=b: h.matmul(
                banks[b][:, 0:ncols], lhsT=wgu[s][:, kc, gq * 128:(gq + 1) * 128], rhs=nT[:, kc, cols],
                start=(kc == 0), stop=(kc == 7)), reads=WALL(s) + [('nT', tt)], writes=[bk(b)])
        return b
    proj_chunk.rot = 0
    evq = [0]

    def evac_copy(dst, src, reads, writes):
        evq[0] += 1
        if evq[0] % 2:
            S.op('act', lambda h: h.activation(out=dst, in_=src, func=AF.Copy), reads=reads, writes=writes)
        else:
            S.op('dve', lambda h: h.tensor_copy(out=dst, in_=src), reads=reads, writes=writes)

    def proj_kv(st, q):
        s = w_acquire(('win', q))
        w_prefetch()
        store = KT if q == 2 else VT
        sres = 'KT' if q == 2 else 'VT'
        for tt in range(2):
            for gq in range(4):
                b = proj_chunk(s, gq, tt)
                T0 = st * ST + tt * 512
                evac_copy(store[:, gq, T0:T0 + 512], banks[b][:, :], [bk(b)], [(sres, st, gq)])

    def proj_q(st):
        s = w_acquire(('win', 1))
        w_prefetch()
        for tt in range(2):
            cols = slice(tt * 512, (tt + 1) * 512)
            for gq in range(4):
                b = proj_chunk(s, gq, tt)
                S.op('act', lambda h, gq=gq, b=b, cols=cols: h.activation(
                    out=QTm[0:64, 0, gq, cols], in_=banks[b][0:64, :], func=AF.Copy),
                    reads=[bk(b)], writes=[('QT', gq)])
                S.op('dve', lambda h, gq=gq, b=b, cols=cols: h.tensor_copy(
                    out=QTm[64:128, 1, gq, cols], in_=banks[b][64:128, :]),
                    reads=[bk(b)], writes=[('QT', gq)])

    def proj_u_send(st):
        s = w_acquire(('win', 0))
        w_prefetch()
        for g in range(4):
            b = proj_chunk(s, g, 1, ncols=16, c0=496)
            S.op('dve', lambda h, g=g, b=b: h.tensor_copy(out=tmp[0][:, g * 16:(g + 1) * 16], in_=banks[b][:, 0:16]),
                 reads=[bk(b)], writes=[('tmp', 0)])
        if st == 2:
            S.op('sp', lambda h: h.dma_start(out=stashU[:, :], in_=tmp[0][:, 0:64]), reads=[('tmp', 0)],
                 writes=['stashU'], dkey='stashU')
            return
        S.op('sp', lambda h: h.dma_start(out=sendU[:, :], in_=tmp[0][:, 0:64]), reads=[('tmp', 0)],
             writes=['sendU'], dkey='sendU')
        S.op('pool', lambda h: h.collective_compute("AllGather", ALU.bypass, replica_groups=RG,
                                                     ins=[sendU.ap().opt()], outs=[gathU.ap().opt()]),
             reads=['sendU'], writes=['gathU'], dkey='ccU', inc=1)

    def recv_halo(slist):
        for s_ in slist:
            for (gt, T, nm) in ((gathK[s_], KT, 'KT'), (gathV[s_], VT, 'VT')):
                for g_ in range(4):
                    S.op('pool', lambda h, gt=gt, T=T, s_=s_, g_=g_: h.indirect_dma_start(
                        out=T[:, g_, s_ * ST:(s_ + 1) * ST], out_offset=None,
                        in_=gt.ap().rearrange("r (g t) -> (r g) t", g=4),
                        in_offset=bass.IndirectOffsetOnAxis(ap=idx_sb[:, g_:g_ + 1], axis=0)),
                        reads=[('gath', nm, s_), 'idx'], writes=[(nm, s_, g_)],
                        dkey=('recv', nm, s_, g_))

    def recv_u():
        S.op('pool', lambda h: h.indirect_dma_start(
            out=carry[:].rearrange("p a b -> p (a b)"), out_offset=None, in_=gathU[:, :],
            in_offset=bass.IndirectOffsetOnAxis(ap=idx_sb[:, 4:5], axis=0)),
            reads=['gathU', 'idx'], writes=[('carry', g_) for g_ in range(4)], dkey='recvU')
        for g in range(4):
            S.op('dve', lambda h, g=g: h.tensor_scalar(out=carry[:, g, :], in0=carry[:, g, :], scalar1=flag[:, 0:1],
                                                        scalar2=None, op0=ALU.mult),
                 reads=[('carry', g), 'flag'], writes=[('carry', g)])

    def carry_from_stash():
        S.op('sp', lambda h: h.dma_start(out=carry[:].rearrange("p a b -> p (a b)"), in_=stashU[:, :]),
             reads=['stashU'], writes=[('carry', g_) for g_ in range(4)], dkey='unstash')

    def send_kv(st, which=None):
        s_ = st - 2
        T0 = st * ST
        for (T, snd, gt, nm) in ((KT, sendK[s_], gathK[s_], 'KT'), (VT, sendV[s_], gathV[s_], 'VT')):
            if which is not None and nm != which:
                continue
            S.op('sp', lambda h, T=T, snd=snd: h.dma_start(
                out=snd.ap().rearrange("p (g t) -> p g t", g=4), in_=T[:, :, T0:T0 + ST]),
                reads=[(nm, st, g_) for g_ in range(4)], writes=[('send', nm, s_)], dkey=('send', nm, s_))
            S.op('pool', lambda h, snd=snd, gt=gt: h.collective_compute(
                "AllGather", ALU.bypass, replica_groups=RG, ins=[snd.ap().opt()], outs=[gt.ap().opt()]),
                reads=[('send', nm, s_)], writes=[('gath', nm, s_)], dkey=('cc', nm, s_), inc=1)

    def spill_h(st, tiles=(0, 1)):
        s_ = st - 2
        for tt in tiles:
            S.op('sp', lambda h, tt=tt: h.dma_start(
                out=h1buf[s_].ap().rearrange("p (c t) -> p c t", c=8)[:, :, tt * 512:(tt + 1) * 512],
                in_=hT[:, :, tt * 512:(tt + 1) * 512]),
                reads=[hres(tt, c) for c in range(8)], writes=[('h1buf', s_, tt)], dkey=('spill', s_, tt))

    def spill_n(st, tiles=(0, 1)):
        s_ = st - 2
        for tt in tiles:
            S.op('sp', lambda h, tt=tt: h.dma_start(
                out=n2buf[s_].ap().rearrange("p (c t) -> p c t", c=8)[:, :, tt * 512:(tt + 1) * 512],
                in_=nT[:, :, tt * 512:(tt + 1) * 512]),
                reads=[('nT', tt)], writes=[('n2buf', s_, tt)], dkey=('spilln', s_, tt))

    def reload_n(st, tiles=(0, 1)):
        s_ = st - 2
        for tt in tiles:
            S.op('sp', lambda h, tt=tt: h.dma_start(
                out=nT[:, :, tt * 512:(tt + 1) * 512],
                in_=n2buf[s_].ap().rearrange("p (c t) -> p c t", c=8)[:, :, tt * 512:(tt + 1) * 512]),
                reads=[('n2buf', s_, tt)], writes=[('nT', tt)], dkey=('reloadn', tt))

    def reload_h(st, tiles=(0, 1)):
        s_ = st - 2
        for tt in tiles:
            S.op('sp', lambda h, tt=tt: h.dma_start(
                out=hT[:, :, tt * 512:(tt + 1) * 512],
                in_=h1buf[s_].ap().rearrange("p (c t) -> p c t", c=8)[:, :, tt * 512:(tt + 1) * 512]),
                reads=[('h1buf', s_, tt)], writes=[hres(tt, c) for c in range(8)], dkey=('reload', tt))

    def proj_u_pool(st):
        s = w_acquire(('win', 0))
        w_prefetch()
        sq_ = w_acquire(('win', 1))
        it = 0
        pend = [None]
        for tt in range(2):
            cols = slice(tt * 512, (tt + 1) * 512)
            for g in range(4):
                w = (2, 4, 8, 16)[g]
                b = proj_chunk(s, g, tt)
                k = it % 2
                it += 1
                S.op('act', lambda h, k=k, b=b: h.activation(out=ub[k][:, 16:528], in_=banks[b][:, :], func=AF.Copy),
                     reads=[bk(b)], writes=[('ub', k)])
                S.op('dve', lambda h, k=k, g=g: h.tensor_copy(out=ub[k][:, 0:16], in_=carry[:, g, :]),
                     reads=[('carry', g)], writes=[('ub', k)])
                S.op('dve', lambda h, k=k, g=g: h.tensor_copy(out=carry[:, g, :], in_=ub[k][:, 512:528]),
                     reads=[('ub', k)], writes=[('carry', g)])
                curt, curres = ub[k], ('ub', k)
                lo = 0
                pi = 0
                step = 1
                while step < w:
                    lo += step
                    nxt, nres = pw[pi % 2], ('pw', pi % 2)
                    S.op('dve', lambda h, curt=curt, nxt=nxt, lo=lo, step=step: h.tensor_tensor(
                        out=nxt[:, lo:528], in0=curt[:, lo:528], in1=curt[:, lo - step:528 - step], op=ALU.add),
                        reads=[curres], writes=[nres])
                    curt, curres = nxt, nres
                    pi += 1
                    step *= 2
                if st == 2 and tt == 0:
                    S.op('dve', lambda h, curt=curt, g=g: h.tensor_tensor(
                        out=curt[:, 16:32], in0=curt[:, 16:32], in1=pcfix[:, g, :], op=ALU.mult),
                        reads=[curres, 'pcfix'], writes=[curres])
                S.op('dve', lambda h, curt=curt, k=k, w=w: h.scalar_tensor_tensor(
                    out=pooled[k][:], in0=curt[:, 16:528], scalar=1.0 / w, in1=ub[k][:, 16:528],
                    op0=ALU.mult, op1=ALU.subtract), reads=[curres, ('ub', k)], writes=[('pooled', k)])
                bq = proj_chunk(sq_, g, tt)
                S.op('act', lambda h, g=g, bq=bq, cols=cols: h.activation(
                    out=QTm[0:64, 0, g, cols], in_=banks[bq][0:64, :], func=AF.Copy),
                    reads=[bk(bq)], writes=[('QT', g)])
                S.op('act', lambda h, g=g, bq=bq, cols=cols: h.activation(
                    out=QTm[64:128, 1, g, cols], in_=banks[bq][64:128, :], func=AF.Copy),
                    reads=[bk(bq)], writes=[('QT', g)])

                def pooled_mm(g=g, k=k, cols=cols):
                    b2 = 4 + k
                    S.op('pe', lambda h: h.matmul(banks[b2][:, :], lhsT=wpool[:, g, :], rhs=pooled[k][:],
                                                  start=True, stop=True),
                         reads=['wpool', ('pooled', k)], writes=[bk(b2)])
                    S.op('act', lambda h: h.activation(
                        out=yT[:, g, cols], in_=banks[b2][:, :], func=AF.Identity, scale=PSC[:, g:g + 1]),
                        reads=[bk(b2), 'gvec'], writes=[('yT', g)])
                if pend[0] is not None:
                    pend[0]()
                pend[0] = pooled_mm
        pend[0]()
        w_prefetch()

    VtAll4 = VtAll[:, :, :].rearrange("p s (a b) -> p s a b", a=4)
    DMv = DM[:, :, :].rearrange("p h (t c) -> p h t c", t=2)

    def attention(st):
        qbase = st * ST
        rot = dict(A=0, B=0, C=0)

        def attn_pair(g):
            blocks = []
            for b in range(8):
                blocks.append((1, qbase + 128 * b, 128, qbase + 128 * b - 128, qbase + 128 * b, 0))
            for r in range(4):
                for b in range(2):
                    q0 = qbase + r + 512 * b
                    blocks.append((4, q0, 128, q0 - 512, q0, 0))
            for r in range(16):
                blocks.append((16, qbase + r, 64, r, 2048 + r, 0 if st == 2 else 64))
            pending = []
            for bi, (d, qT0, nq, pT0, dT0, qoff) in enumerate(blocks):
                nh = sum(1 for i in range(128) if pT0 + d * i < 2048)
                var = {128: 'A', 0: 'C'}[nh]
                cpair = rot['C'] % 3
                rot['C'] += 1
                vds = 4 + 2 * cpair
                if var == 'C':
                    vps, valid, vres = 3 + 2 * cpair, None, None
                else:
                    vps, valid, vres = rot['A'] % 3, flag, 'flag'
                    rot['A'] += 1
                sts = sorted(set([(pT0 + d * i) // ST for i in (0, 127)] + [(dT0 + d * i) // ST for i in (0, 127)]))
                sts = list(range(sts[0], sts[-1] + 1))
                if st == 3:
                    sts = [x_ for x_ in sts if x_ != 0]
                vtreads = [('VT', st2, g) for st2 in sts] + ['identB']
                kreads = [('KT', st2, g) for st2 in sts] + [('QT', g)]
                tb = 6 + (bi % 2)
                sb = bi % 4
                pi = bi % 4
                pt, ptres = PT[pi], ('PT', pi)
                pkeys = slice(pT0, pT0 + 127 * d + 1, d)
                dsl = slice(dT0, dT0 + 127 * d + 1, d)

                def dk(T, g=g, dsl=dsl):
                    return T[:, g, dsl]
                ql0 = qT0 - qbase
                qcols = slice(ql0, ql0 + (nq - 1) * d + 1, d)
                idx0 = 2 * g + 4 - {1: 0, 4: 2, 16: 4}[d]
                S.op('pe', lambda h, tb=tb, pkeys=pkeys, g=g: h.transpose(
                    out=banksB[tb][:, 0:128], in_=VT[:, g, pkeys], identity=identB[:]),
                    reads=vtreads, writes=[bk(tb)])
                S.op('pe', lambda h, tb=tb, dk=dk: h.transpose(
                    out=banksB[tb][:, 128:256], in_=dk(VT), identity=identB[:]),
                    reads=vtreads, writes=[bk(tb)])
                if var == 'C':
                    S.op('act', lambda h, tb=tb, vps=vps: h.activation(
                        out=VtAll4[:, vps:vps + 2, 0:4:3, :],
                        in_=banksB[tb][:, 0:256].rearrange("p (s a b) -> p s a b", s=2, a=2),
                        func=AF.Copy), reads=[bk(tb)], writes=[('Vt', vps), ('Vt', vds)])
                else:
                    S.op('act', lambda h, tb=tb, vps=vps, valid=valid: h.activation(
                        out=VtAll4[:, vps, 0:4:3, :], in_=banksB[tb][:, 0:128].rearrange("p (a b) -> p a b", a=2),
                        func=AF.Identity, scale=valid[:, 0:1]), reads=[bk(tb), vres], writes=[('Vt', vps)])
                    S.op('act', lambda h, tb=tb, vds=vds: h.activation(
                        out=VtAll4[:, vds, 0:4:3, :], in_=banksB[tb][:, 128:256].rearrange("p (a b) -> p a b", a=2),
                        func=AF.Copy), reads=[bk(tb)], writes=[('Vt', vds)])
                S.op('pe', lambda h, sb=sb, pkeys=pkeys, qcols=qcols, nq=nq, g=g: h.matmul(
                    banks[sb][:, 0:2 * nq], lhsT=KT[:, g, pkeys], rhs=QTm[:, :, g, qcols],
                    start=True, stop=True), reads=kreads, writes=[bk(sb)])
                S.op('pe', lambda h, sb=sb, dk=dk, qcols=qcols, nq=nq, g=g: h.matmul(
                    banks[sb][:, 2 * nq:4 * nq], lhsT=dk(KT), rhs=QTm[:, :, g, qcols],
                    start=True, stop=True), reads=kreads, writes=[bk(sb)])
                S.op('act', lambda h, sb=sb, pt=pt, nq=nq: h.activation(
                    out=pt[:, 0:4 * nq], in_=banks[sb][:, 0:4 * nq], func=AF.Exp, scale=0.125),
                    reads=[bk(sb)], writes=[ptres])
                S.op('dve', lambda h, pt=pt, nq=nq, idx0=idx0, qoff=qoff: h.tensor_tensor(
                    out=pt[:, 0:4 * nq].rearrange("p (t h c) -> p t h c", t=2, h=2),
                    in0=pt[:, 0:4 * nq].rearrange("p (t h c) -> p t h c", t=2, h=2),
                    in1=DMv[:, idx0:idx0 + 2, :, qoff:qoff + nq].rearrange("p h t c -> p t h c"), op=ALU.mult),
                    reads=[ptres, ('DM', idx0), ('DM', idx0 + 1)], writes=[ptres])

                def pv(pt=pt, ptres=ptres, vps=vps, vds=vds, nq=nq, qcols=qcols, first=(d == 1)):
                    for hh in range(2):
                        ob = 4 + hh
                        S.op('pe', lambda h, hh=hh, ob=ob: h.matmul(
                            banks[ob][:, 0:nq], lhsT=VtAll[:, vps, hh * 128:(hh + 1) * 128],
                            rhs=pt[:, hh * nq:(hh + 1) * nq], start=True, stop=False),
                            reads=[('Vt', vps), ptres], writes=[bk(ob)])
                        S.op('pe', lambda h, hh=hh, ob=ob: h.matmul(
                            banks[ob][:, 0:nq], lhsT=VtAll[:, vds, hh * 128:(hh + 1) * 128],
                            rhs=pt[:, 2 * nq + hh * nq:2 * nq + (hh + 1) * nq], start=False, stop=True),
                            reads=[('Vt', vds), ptres], writes=[bk(ob)])
                        if first:
                            S.op('dve', lambda h, hh=hh, ob=ob: h.tensor_copy(
                                out=Oacc[:, hh, qcols], in_=banks[ob][:, 0:nq]),
                                reads=[bk(ob)], writes=[('Oacc', hh)])
                        else:
                            S.op('dve', lambda h, hh=hh, ob=ob: h.tensor_tensor(
                                out=Oacc[:, hh, qcols], in0=banks[ob][:, 0:nq], in1=Oacc[:, hh, qcols], op=ALU.add),
                                reads=[bk(ob), ('Oacc', hh)], writes=[('Oacc', hh)])
                pending.append(pv)
                if len(pending) > 2:
                    pending.pop(0)()
            while pending:
                pending.pop(0)()
            for hf in range(2):
                cs = slice(hf * 512, (hf + 1) * 512)
                S.op('act', lambda h, cs=cs: h.activation(out=rec[0:64, :], in_=Oacc[64:128, 0, cs], func=AF.Ln),
                     reads=[('Oacc', 0)], writes=['rec'])
                S.op('act', lambda h, cs=cs: h.activation(out=rec[64:128, :], in_=Oacc[0:64, 1, cs], func=AF.Ln),
                     reads=[('Oacc', 1)], writes=['rec'])
                S.op('act', lambda h: h.activation(out=rec[:, :], in_=rec[:, :], func=AF.Exp, scale=-1.0),
                     reads=['rec'], writes=['rec'])
                S.op('dve', lambda h, g=g, cs=cs: h.tensor_tensor(
                    out=yT[0:64, 4 + g, cs], in0=Oacc[0:64, 0, cs], in1=rec[0:64, :], op=ALU.mult),
                    reads=[('Oacc', 0), 'rec'], writes=[('yT', 4 + g)])
                S.op('dve', lambda h, g=g, cs=cs: h.tensor_tensor(
                    out=yT[64:128, 4 + g, cs], in0=Oacc[64:128, 1, cs], in1=rec[64:128, :], op=ALU.mult),
                    reads=[('Oacc', 1), 'rec'], writes=[('yT', 4 + g)])
        for g in range(4):
            attn_pair(g)

    def wout_phase(after_t0=None):
        s0 = w_acquire(('wout', 0))
        w_prefetch()
        s1 = w_acquire(('wout', 1))
        sl = [s0, s1]
        for tt in range(2):
            cols = slice(tt * 512, (tt + 1) * 512)
            for c in range(8):
                b = 4 + (c % 2)
                for kc in range(8):
                    S.op('pe', lambda h, kc=kc, c=c, b=b, cols=cols: h.matmul(
                        banks[b][:, :], lhsT=wov[sl[kc // 4]][:, kc % 4, c * 128:(c + 1) * 128], rhs=yT[:, kc, cols],
                        start=(kc == 0), stop=(kc == 7)),
                        reads=WALL(sl[kc // 4]) + [('yT', kc)], writes=[bk(b)])
                S.op('dve', lambda h, c=c, b=b, cols=cols: h.scalar_tensor_tensor(
                    out=hT[:, c, cols], in0=banks[b][:, :], scalar=GT2[:, c:c + 1], in1=hT[:, c, cols],
                    op0=ALU.mult, op1=ALU.add), reads=[bk(b), mres(5), hres(tt, c)], writes=[hres(tt, c)])
            if tt == 0 and after_t0 is not None:
                after_t0()
        w_prefetch()

    def final_tile(st, tt):
        if True:
            norm_stats(tt)
            for i in range(4):
                c0 = tt * 512 + i * 128
                k = (tt * 4 + i) % 2
                for c in range(8):
                    S.op('dve', lambda h, c=c, c0=c0, i=i: h.scalar_tensor_tensor(
                        out=ot[:, c, :], in0=hT[:, c, c0:c0 + 128], scalar=GFIN[:, c:c + 1],
                        in1=rstd[:, i * 128:(i + 1) * 128], op0=ALU.mult, op1=ALU.mult),
                        reads=[hres(tt, c), 'gvec', 'rstd'], writes=[('ot', c)])
                for half in range(2):
                    b = 6 + half
                    for cc in range(4):
                        c = 4 * half + cc
                        S.op('pe', lambda h, c=c, cc=cc, b=b: h.transpose(
                            out=banks[b][:, cc * 128:(cc + 1) * 128], in_=ot[:, c, :], identity=identF[:]),
                            reads=[('ot', c), 'identF'], writes=[bk(b)])
                    dst = ostage[k][:, half * 512:(half + 1) * 512]
                    if half == 0:
                        S.op('act', lambda h, dst=dst, b=b: h.activation(out=dst, in_=banks[b][:, :], func=AF.Copy),
                             reads=[bk(b)], writes=[('ostage', k)])
                    else:
                        S.op('dve', lambda h, dst=dst, b=b: h.tensor_copy(out=dst, in_=banks[b][:, :]),
                             reads=[bk(b)], writes=[('ostage', k)])
                r0 = (st - 2) * ST + c0
                S.op('sp', lambda h, k=k, r0=r0: h.dma_start(out=out_d[r0:r0 + 128, :], in_=ostage[k][:]),
                     reads=[('ostage', k)], dkey=('ostage', k))

    for st in (3, 2):
        load_x(st, after_t0=lambda: (mod_upfront(), norm_to_nT(0, G1, SH1, 'G1', mres(0))))
        if st == 2:
            send_kv(3)
        norm_to_nT(1, G1, SH1, 'G1', mres(0))
        ffn(1, GT1, 'GT1',
            after_t0=lambda st=st: (norm_to_nT(0, G2, SH2, 'G2', mres(3)), spill_h(st, (0,)),
                                    spill_n(st, (0,)) if st == 2 else None,
                                    reload_h(3, (0,)) if st == 2 else None),
            iter_hook=(lambda it: (mod_blocks(4 if it == 0 else 2),
                                   kv_zero_part(it - 14) if 14 <= it < 18 else None,
                                   qtm_zero() if it == 19 else None)) if st == 3 else None)
        norm_to_nT(1, G2, SH2, 'G2', mres(3))
        spill_h(st, (1,))
        if st == 2:
            spill_n(st, (1,))
        if st == 2:
            reload_h(3, (1,))
        proj_kv(st, 2)
        proj_kv(st, 3)
        proj_u_send(st)
    w_prefetch(ahead=2)
    send_kv(2)
    norm_to_nT(0, G2, SH2, 'G2', mres(3))
    norm_to_nT(1, G2, SH2, 'G2', mres(3))
    for st in (3, 2):
        if st == 2:
            recv_u()
        else:
            carry_from_stash()
        proj_u_pool(st)
        recv_halo((1,) if st == 3 else (0,))
        attention(st)
        wout_phase(after_t0=lambda: norm_to_nT(0, G3, SH3, 'G3', mres(6)))
        norm_to_nT(1, G3, SH3, 'G3', mres(6))
        if st == 3:
            ffn(2, GT3, 'GT3', after_t0=lambda: (final_tile(3, 0), reload_h(2, (0,)), reload_n(2, (0,))))
            final_tile(3, 1)
            reload_h(2, (1,))
            reload_n(2, (1,))
        else:
            ffn(2, GT3, 'GT3', after_t0=lambda: final_tile(2, 0))
            final_tile(2, 1)
    assert W['acq'] == len(wseq), (W, len(wseq))
    S.emit(nc, final_wait_keys=[('ostage', 0), ('ostage', 1)])
    return nc


def _host_consts():
    ident = np.eye(128, dtype=np.float32)
    ki = np.arange(128)[:, None].astype(np.float64)
    qi = np.arange(128)[None, :].astype(np.float64)
    dm = np.zeros((128, 12, 256), dtype=np.float32)
    for idx in range(12):
        a = 2.0 ** (3 - idx)
        e = idx - 8
        prev = np.where(ki >= qi, np.exp(-a * np.clip(qi + 128 - ki, 0, 256)), 0.0)
        diag = np.where(ki <= qi, np.exp(-a * np.clip(qi - ki, 0, 256)), 0.0)
        dm[:, e + 8, 0:128] = prev
        dm[:, e + 8, 128:256] = diag
    return ident, dm.reshape(128, 12 * 256)


def _colT(v, n):
    return np.ascontiguousarray(np.asarray(v, dtype=np.float32).reshape(n, 128).T)


_NC_CACHE = {}


def kernel(x, c, w_ada, b_ada, g_ffn1, w1_gate, w1_up, w1_down, g_mix, w_in, w_pool,
           pool_scale, w_out, g_ffn2, w2_gate, w2_up, w2_down, g_final):
    f = lambda a: np.ascontiguousarray(np.asarray(a, dtype=np.float32))
    x = f(x)
    c = f(c)
    ident, dm = _host_consts()
    gvec = np.concatenate([_colT(f(g_ffn1)[0], 8), _colT(f(g_mix)[0], 8), _colT(f(g_ffn2)[0], 8),
                           _colT(f(g_final), 8), _colT(f(pool_scale)[0], 4)], axis=1)
    shared = {
        "w_ada": f(w_ada)[0], "bada": f(b_ada)[0].reshape(1, 9 * D), "gvec": np.ascontiguousarray(gvec),
        "w1_gate": f(w1_gate)[0], "w1_up": f(w1_up)[0], "w1_down": f(w1_down)[0],
        "w2_gate": f(w2_gate)[0], "w2_up": f(w2_up)[0], "w2_down": f(w2_down)[0],
        "w_in": f(w_in)[0], "w_pool": f(w_pool)[0], "w_out": f(w_out)[0],
        "ident": ident, "dmask": dm,
    }
    in_maps = []
    for core in range(NCORES):
        b, j = core // 4, core % 4
        x4 = np.ascontiguousarray(x[b, j * OWN:(j + 1) * OWN])
        base = max(j - 1, 0) * 128 + np.arange(128)
        idx = np.stack([4 * base + 0, 4 * base + 1, 4 * base + 2, 4 * base + 3, base], axis=1).astype(np.int32)
        flag = np.full((128, 1), 1.0 if j > 0 else 0.0, dtype=np.float32)
        pcfix = np.ones((128, 4, 16), dtype=np.float32)
        if j == 0:
            for g, w in enumerate((2, 4, 8, 16)):
                for t in range(16):
                    pcfix[:, g, t] = float(w) / float(min(t + 1, w))
        m = dict(shared)
        m.update({"x4": x4, "cT": _colT(c[b], 8), "flag": flag, "pcfix": pcfix.reshape(128, 64), "idx": idx})
        in_maps.append(m)
    if "nc" not in _NC_CACHE:
        _NC_CACHE["nc"] = build_program()
    nc = _NC_CACHE["nc"]
    res = run_bass_kernel_spmd(nc, in_maps, core_ids=list(range(NCORES)))
    out = np.zeros((2, 4 * OWN, D), dtype=np.float32)
    for core in range(NCORES):
        b, j = core // 4, core % 4
        out[b, j * OWN:(j + 1) * OWN] = res.results[core]["out"]
    return out
```

```python
import numpy as np
import concourse.bass as bass
import concourse.mybir as mybir
from concourse.bass_utils import run_bass_kernel_spmd

F32 = mybir.dt.float32
BF16 = mybir.dt.bfloat16
AF = mybir.ActivationFunctionType
ALU = mybir.AluOpType

D = 1024
DFF = 2816
NCORES = 8
OWN = 2048
TOK = 4096
ST = 1024
NGRP = 11
EPS = 1e-6
STAGES = ("ffn1", "mix", "ffn2")


class Sched:
    CE = ('pe', 'act', 'dve', 'pool', 'sp')

    def __init__(self):
        self.ops = []
        self.lastw = {}
        self.readers = {}
        self.region = {}
        self.regmembers = {}
        self.total_keys = set()

    def set_region(self, res, region, setname):
        self.region[res] = (region, setname)
        self.regmembers.setdefault(region, []).append(res)

    def _conflicts(self, w):
        if w not in self.region:
            return ()
        reg, sn = self.region[w]
        return [r for r in self.regmembers[reg] if self.region[r][1] != sn]

    def op(self, eng, fn, reads=(), writes=(), dkey=None, after=(), inc=16):
        deps = set(after)
        for r in reads:
            if r in self.lastw:
                deps.add(self.lastw[r])
        for w in writes:
            for x in [w] + list(self._conflicts(w)):
                if x in self.lastw:
                    deps.add(self.lastw[x])
                deps.update(self.readers.get(x, {}).values())
        i = len(self.ops)
        deps.discard(i)
        self.ops.append(dict(eng=eng, fn=fn, deps=deps, dkey=dkey, inc=inc))
        for r in reads:
            d = self.readers.setdefault(r, {})
            d[('dma', i) if dkey is not None else eng] = i
        for w in writes:
            self.lastw[w] = i
            self.readers[w] = {}
        return i

    def emit(self, nc, final_wait_keys=()):
        ops = self.ops

        def fdeps(o):
            out = []
            for d in o['deps']:
                x = ops[d]
                if x['dkey'] is None and x['eng'] == 'pe' and o['eng'] == 'pe' and o['dkey'] is None:
                    continue
                if x['dkey'] is not None and x['dkey'] in self.total_keys and x['dkey'] == o['dkey']:
                    continue
                out.append(d)
            return out
        need = set()
        for o in ops:
            o['fd'] = fdeps(o)
            need.update(o['fd'])
        cnt = {}
        dcnt = {}
        for i, o in enumerate(ops):
            if o['dkey'] is not None:
                k = o['dkey']
                dcnt[k] = dcnt.get(k, 0) + o['inc']
                o['sig'] = (('d', k), dcnt[k])
            elif i in need:
                e = o['eng']
                cnt[e] = cnt.get(e, 0) + 1
                o['sig'] = (('e', e), cnt[e])
            else:
                o['sig'] = None
        for o in ops:
            if o['dkey'] in self.total_keys:
                o['sig'] = (('d', o['dkey']), dcnt[o['dkey']])
        semnames = sorted(set(o['sig'][0] for o in ops if o['sig'] is not None), key=str)
        with nc.cleanup_on_exit():
            self._emit_body(nc, ops, semnames, per_eng_init=True, final_wait_keys=final_wait_keys, dcnt=dcnt)

    def _emit_body(self, nc, ops, semnames, per_eng_init, final_wait_keys, dcnt):
        sems = {}
        for n, sn in enumerate(semnames):
            sems[sn] = nc.alloc_semaphore("s%d" % n)
        per_eng = {e: [] for e in self.CE}
        for i, o in enumerate(ops):
            per_eng[o['eng']].append(i)
        finals = [(('d', k), dcnt[k]) for k in final_wait_keys if k in dcnt]

        def run_engine(e, h):
            waited = {}
            for i in per_eng[e]:
                o = ops[i]
                reqs = {}
                for d in o['fd']:
                    sn, v = ops[d]['sig']
                    if reqs.get(sn, 0) < v:
                        reqs[sn] = v
                for sn, v in reqs.items():
                    if waited.get(sn, 0) >= v:
                        continue
                    h.wait_ge(sems[sn], v)
                    waited[sn] = v
                ins = o['fn'](h)
                if o['sig'] is not None:
                    ins.then_inc(sems[o['sig'][0]], o['inc'] if o['dkey'] is not None else 1)
            if e == 'sp':
                for sn, v in finals:
                    h.wait_ge(sems[sn], v)
        with nc.Block() as block:
            @block.tensor
            def _(h):
                run_engine('pe', h)

            @block.scalar
            def _(h):
                run_engine('act', h)

            @block.vector
            def _(h):
                run_engine('dve', h)

            @block.gpsimd
            def _(h):
                run_engine('pool', h)

            @block.sync
            def _(h):
                run_engine('sp', h)


def weight_sequence():
    seq = []
    for s in range(2):
        seq += [('ffn', 1, gi) for gi in range(NGRP)]
        seq += [('win', 2), ('win', 3), ('win', 0)]
    for s in range(2):
        seq += [('win', 0), ('win', 1), ('wout', 0), ('wout', 1)]
        seq += [('ffn', 2, gi) for gi in range(NGRP)]
    return seq


def build_program():
    nc = bass.Bass("TRN2", target_bir_lowering=False)
    S = Sched()

    def din(name, shape):
        return nc.dram_tensor(name, list(shape), F32, kind="ExternalInput").ap()
    x4 = din("x4", [OWN, D])
    idx_d = nc.dram_tensor("idx", [128, 5], mybir.dt.int32, kind="ExternalInput").ap()
    cT_d = din("cT", [128, 8])
    flag_d = din("flag", [128, 1])
    pcfix_d = din("pcfix", [128, 64])
    w_ada = din("w_ada", [D, 9 * D])
    bada_d = din("bada", [1, 9 * D])
    gvec_d = din("gvec", [128, 36])
    wg_d = {1: din("w1_gate", [D, DFF]), 2: din("w2_gate", [D, DFF])}
    wu_d = {1: din("w1_up", [D, DFF]), 2: din("w2_up", [D, DFF])}
    wd_d = {1: din("w1_down", [DFF, D]), 2: din("w2_down", [DFF, D])}
    w_in = din("w_in", [D, 2 * D])
    w_pool = din("w_pool", [4, 128, 128])
    w_out = din("w_out", [D, D])
    ident_d = din("ident", [128, 128])
    dm_d = din("dmask", [128, 12 * 256])
    out_d = nc.dram_tensor("out", [OWN, D], F32, kind="ExternalOutput").ap()

    h1buf = [nc.dram_tensor("h1buf%d" % s_, [128, 8 * ST], F32) for s_ in range(2)]
    n2buf = [nc.dram_tensor("n2buf%d" % s_, [128, 8 * ST], BF16) for s_ in range(2)]
    sendK = [nc.dram_tensor("sendK%d" % s_, [128, 4 * ST], BF16) for s_ in range(2)]
    sendV = [nc.dram_tensor("sendV%d" % s_, [128, 4 * ST], BF16) for s_ in range(2)]
    gathK = [nc.dram_tensor("gathK%d" % s_, [512, 4 * ST], BF16) for s_ in range(2)]
    gathV = [nc.dram_tensor("gathV%d" % s_, [512, 4 * ST], BF16) for s_ in range(2)]
    sendU = nc.dram_tensor("sendU", [128, 64], F32)
    stashU = nc.dram_tensor("stashU", [128, 64], F32)
    gathU = nc.dram_tensor("gathU", [512, 64], F32)
    RG = [[0, 1, 2, 3], [4, 5, 6, 7]]

    cur = [16512]

    def alloc(name, shape, dt, off=None):
        nbytes = int(np.prod(shape[1:])) * (4 if dt == F32 else 2)
        if off is None:
            off = cur[0]
            cur[0] += (nbytes + 31) // 32 * 32
        return nc.alloc_sbuf_tensor_at(name, list(shape), dt, offset=off)
    KT = alloc("KT", [128, 4, TOK], BF16)
    VT = alloc("VT", [128, 4, TOK], BF16)
    DM = alloc("DM", [128, 12, 256], BF16)
    wpool = alloc("wpool", [128, 4, 128], BF16)
    VtAll = alloc("VtAll", [128, 9, 256], BF16)
    identF = alloc("identF", [128, 128], F32)
    identB = alloc("identB", [128, 128], BF16)
    onesB = alloc("onesB", [128, 128], BF16)
    modT = alloc("modT", [128, 72], F32)
    gvec = alloc("gvec_sb", [128, 36], F32)
    G1 = alloc("G1", [128, 8], F32)
    GT1 = alloc("GT1", [128, 8], F32)
    G2 = alloc("G2", [128, 8], F32)
    G3 = alloc("G3", [128, 8], F32)
    GT3 = alloc("GT3", [128, 8], F32)
    flag = alloc("flag_sb", [128, 1], F32)
    idx_sb = alloc("idx_sb", [128, 5], mybir.dt.int32)
    onesv = alloc("onesv", [128, 1], F32)
    epsv = alloc("epsv", [128, 1], F32)
    csb = alloc("csb", [128, 8], F32)
    silc = alloc("silc", [128, 8], BF16)
    pcfix = alloc("pcfix_sb", [128, 4, 16], F32)
    carry = alloc("carry", [128, 4, 16], F32)
    one11 = alloc("one11", [1, 1], F32)
    hT = alloc("hT", [128, 8, ST], F32)
    Wbase = cur[0]
    cur[0] += 2 * 12288
    wgu = [alloc("wgu%d" % s, [128, 8, 512], BF16, off=Wbase + s * 12288) for s in range(2)]
    wdn = [alloc("wdn%d" % s, [128, 2, 1024], BF16, off=Wbase + s * 12288 + 8192) for s in range(2)]
    wav = [alloc("wav%d" % s, [128, 8, 256], F32, off=Wbase + s * 12288) for s in range(2)]
    wov = [alloc("wov%d" % s, [128, 4, 1024], BF16, off=Wbase + s * 12288) for s in range(2)]
    R0 = cur[0]
    nT = alloc("nT", [128, 8, ST], BF16, off=R0)
    yT = alloc("yT", [128, 8, ST], BF16, off=R0 + 16384)
    QTm = alloc("QTm", [128, 2, 4, ST], BF16, off=R0 + 32768)
    o = R0 + 49152
    sq = [alloc("sq%d" % i, [128, 512], BF16, off=o + 1024 * i) for i in range(2)]
    rstd = alloc("rstd", [128, 512], F32, off=o + 2048)
    std = rstd
    tmp = [alloc("tmp%d" % i, [128, 512], F32, off=o + 4096 + 2048 * i) for i in range(2)]
    o = R0 + 57344
    actT = [[alloc("actT%d%d" % (a, j), [128, 512], BF16, off=o + 2048 * a + 1024 * j) for j in range(2)]
            for a in range(2)]
    sg = [alloc("sg%d" % j, [128, 512], BF16, off=o + 4096 + 1024 * j) for j in range(2)]
    PT = [alloc("PT%d" % i, [128, 512], BF16, off=o + 1024 * i) for i in range(4)]
    rec = alloc("rec", [128, 512], F32, off=o + 4096)
    o = R0 + 63488
    ub = [alloc("ub%d" % i, [128, 528], F32, off=o + 2112 * i) for i in range(2)]
    pw = [alloc("pw%d" % i, [128, 528], F32, off=o + 4224 + 2112 * i) for i in range(2)]
    pooled = [alloc("pooled%d" % i, [128, 512], BF16, off=o + 8448 + 1024 * i) for i in range(2)]
    Oacc = alloc("Oacc", [128, 2, ST], F32, off=o)
    ot = alloc("ot", [128, 8, 128], F32, off=o)
    ostage = [alloc("ostage%d" % i, [128, D], F32, off=o + 4096 + 4096 * i) for i in range(2)]
    xs = [alloc("xs%d" % i, [128, D], F32, off=o + 4096 * i) for i in range(2)]
    modrow = [alloc("modrow%d" % i, [1, 256], F32, off=R0 + 32768 + 1024 * i) for i in range(2)]
    brow = [alloc("brow%d" % i, [1, 256], F32, off=R0 + 32768 + 2048 + 1024 * i) for i in range(4)]
    wa2 = [alloc("wa2_%d" % i, [128, 8, 256], BF16, off=R0 + 16384 + 4096 * i) for i in range(4)]
    assert o + 12288 <= 229344, (o + 12288)
    print("SBUF slack bytes:", 229344 - (o + 12288))

    for a_ in range(2):
        for j in range(2):
            S.set_region(('actT', a_, j), 'R2', 'ffn')
    for j in range(2):
        S.set_region(('sg', j), 'R2', 'ffn')
    for i in range(4):
        S.set_region(('PT', i), 'R2', 'attn')
    S.set_region('rec', 'R2', 'attn')
    for i in range(2):
        S.set_region(('ub', i), 'R3', 'pool')
        S.set_region(('pw', i), 'R3', 'pool')
        S.set_region(('pooled', i), 'R3', 'pool')
        S.set_region(('ostage', i), 'R3', 'fin')
        S.set_region(('xs', i), 'R3', 'xs')
        S.set_region(('modrow', i), 'RQ', 'mod')
    for i in range(4):
        S.set_region(('brow', i), 'RQ', 'mod')
        S.set_region(('ada', i), 'RY', 'ada')
    for g_ in range(4):
        S.set_region(('QT', g_), 'RQ', 'qt')
    for c_ in range(8):
        S.set_region(('yT', c_), 'RY', 'yt')
    for hh in range(2):
        S.set_region(('Oacc', hh), 'R3', 'attn')
    for c in range(8):
        S.set_region(('ot', c), 'R3', 'fin')

    banks = [nc.alloc_psum_tensor("bk%d" % i, [128, 512], F32) for i in range(8)]
    banksB = [b[:, :].bitcast(BF16) for b in banks]

    def bk(i):
        return ('bk', i)

    wada_v = w_ada.rearrange("(kc p) n -> p kc n", p=128)

    S.total_keys.add('const')
    S.total_keys.add('constc')

    def cload(dst, src, res):
        S.op('sp', lambda h: h.dma_start(out=dst, in_=src), writes=[res], dkey='const')
    cload(identF[:], ident_d[:], 'identF')
    cload(csb[:], cT_d[:], 'csb')
    cload(gvec[:], gvec_d[:], 'gvec')
    cload(flag[:], flag_d[:], 'flag')
    cload(idx_sb[:], idx_d[:], 'idx')
    cload(pcfix[:].rearrange("p a b -> p (a b)"), pcfix_d[:], 'pcfix')
    for a0 in range(12):
        S.op('pool', lambda h, a0=a0: h.dma_start(out=DM[:, a0, :], in_=dm_d[:, a0 * 256:(a0 + 1) * 256]),
             writes=[('DM', a0)], dkey='constc')
    S.op('pool', lambda h: h.dma_start(out=wpool[:], in_=w_pool.rearrange("g c d -> c g d")),
         writes=['wpool'], dkey='constc')
    S.op('pool', lambda h: h.dma_start(out=identB[:], in_=ident_d[:]), writes=['identB'], dkey='constc')
    S.op('dve', lambda h: h.memset(onesB[:], 1.0), writes=['onesB'])
    S.op('dve', lambda h: h.memset(onesv[:], 1.0), writes=['onesv'])
    S.op('dve', lambda h: h.memset(epsv[:], EPS), writes=['epsv'])
    S.op('dve', lambda h: h.memset(one11[:], 1.0), writes=['one11'])
    for i in range(3, 9):
        S.op('dve', lambda h, i=i: h.memset(VtAll[:, i, :], 1.0), writes=[('Vt', i)])
    for i in range(3):
        for hf in range(2):
            S.op('dve', lambda h, i=i, hf=hf: h.tensor_scalar(
                out=VtAll[:, i, hf * 128:(hf + 1) * 128], in0=onesB[:], scalar1=flag[:, 0:1], scalar2=None,
                op0=ALU.mult), reads=['onesB', 'flag'], writes=[('Vt', i)])
    def kv_zero_part(g_):
        S.op('dve', lambda h: h.memset(KT[:, g_, 3072:4096], 0.0), writes=[('KT', 3, g_)])
        S.op('dve', lambda h: h.memset(VT[:, g_, 3072:4096], 0.0), writes=[('VT', 3, g_)])
        S.op('dve', lambda h: h.memset(KT[:, g_, 0:1024], 0.0), writes=[('KT', 0, g_)])
        S.op('dve', lambda h: h.memset(VT[:, g_, 0:1024], 0.0), writes=[('VT', 0, g_)])
    S.op('dve', lambda h: h.memset(QTm[64:128, 0, :, :], 0.0), writes=[('QT', g) for g in range(4)])
    S.op('dve', lambda h: h.memset(QTm[0:64, 1, :, :], 0.0), writes=[('QT', g) for g in range(4)])

    wseq = weight_sequence()
    W = dict(issued=0, acq=0)
    win_v = w_in.rearrange("(kc p) n -> p kc n", p=128)
    wg_v = {k: v.rearrange("(kc p) n -> p kc n", p=128) for k, v in wg_d.items()}
    wu_v = {k: v.rearrange("(kc p) n -> p kc n", p=128) for k, v in wu_d.items()}
    WALL = lambda s: [('W', s, 'g'), ('W', s, 'u'), ('W', s, 'd')]

    def w_issue(k):
        desc = wseq[k]
        s = k % 2
        if False:
            pass
        elif desc[0] == 'ffn':
            which, gi = desc[1], desc[2]
            f0 = gi * 256
            S.op('pool', lambda h: h.dma_start(out=wgu[s][:, :, 0:256], in_=wg_v[which][:, :, f0:f0 + 256]),
                 writes=[('W', s, 'g')], dkey=('W', s, 'g'))
            S.op('pool', lambda h: h.dma_start(out=wgu[s][:, :, 256:512], in_=wu_v[which][:, :, f0:f0 + 256]),
                 writes=[('W', s, 'u')], dkey=('W', s, 'u'))
            S.op('pool', lambda h: h.dma_start(
                out=wdn[s][:], in_=wd_d[which][f0:f0 + 256, :].rearrange("(j p) n -> p j n", p=128)),
                writes=[('W', s, 'd')], dkey=('W', s, 'd'))
        elif desc[0] == 'win':
            q = desc[1]
            S.op('pool', lambda h: h.dma_start(out=wgu[s][:], in_=win_v[:, :, q * 512:(q + 1) * 512]),
                 writes=[('W', s, 'g'), ('W', s, 'u')], dkey=('W', s, 'g'))
        elif desc[0] == 'wout':
            hf = desc[1]
            S.op('pool', lambda h: h.dma_start(
                out=wov[s][:], in_=w_out[hf * 512:(hf + 1) * 512, :].rearrange("(kc p) n -> p kc n", p=128)),
                writes=[('W', s, 'g'), ('W', s, 'u')], dkey=('W', s, 'g'))
        else:
            raise ValueError(desc)

    def w_acquire(desc):
        k = W['acq']
        assert wseq[k] == desc, (k, wseq[k], desc)
        if W['issued'] <= k:
            w_issue(k)
            W['issued'] = k + 1
        W['acq'] = k + 1
        return k % 2

    def w_prefetch(ahead=1):
        k = W['issued']
        if k < len(wseq) and k <= W['acq'] + ahead - 1:
            w_issue(k)
            W['issued'] = k + 1

    S.op('act', lambda h: h.activation(out=silc[:], in_=csb[:], func=AF.Silu), reads=['csb'], writes=['silc'])
    modst = dict(next=0, pend=None)

    def mres(s_):
        return ('modT', s_)

    def gmake(dst, sec, g_off, res):
        S.op('dve', lambda h: h.scalar_tensor_tensor(out=dst[:], in0=modT[:, 8 * sec:8 * sec + 8], scalar=1.0,
                                                     in1=gvec[:, g_off:g_off + 8], op0=ALU.add, op1=ALU.mult),
             reads=[mres(sec), 'gvec'], writes=[res])

    def mod_cols(b):
        for jj in range(2):
            col = 256 + (2 * b + jj) % 8
            S.op('pe', lambda h, b=b, jj=jj, col=col: h.matmul(
                banks[7][:, col:col + 1], lhsT=modrow[b % 2][0:1, jj * 128:(jj + 1) * 128],
                rhs=one11[0:1, 0:1], start=True, stop=True),
                reads=[('modrow', b % 2), 'one11'], writes=[bk(7)])
        if b % 4 == 3:
            sec = b // 4
            S.op('dve', lambda h, sec=sec: h.tensor_copy(out=modT[:, 8 * sec:8 * sec + 8], in_=banks[7][:, 256:264]),
                 reads=[bk(7)], writes=[mres(sec)])
            if sec == 1:
                gmake(G1, 1, 0, 'G1')
            elif sec == 2:
                S.op('dve', lambda h: h.tensor_scalar(out=GT1[:], in0=modT[:, 16:24], scalar1=0.5, scalar2=None,
                                                      op0=ALU.mult), reads=[mres(2)], writes=['GT1'])
            elif sec == 4:
                gmake(G2, 4, 8, 'G2')
            elif sec == 7:
                gmake(G3, 7, 16, 'G3')
            elif sec == 8:
                S.op('dve', lambda h: h.tensor_scalar(out=GT3[:], in0=modT[:, 64:72], scalar1=0.5, scalar2=None,
                                                      op0=ALU.mult), reads=[mres(8)], writes=['GT3'])

    def mod_issue(b):
        if b >= 36 or b < modst.get('issued', 0):
            return
        assert b == modst.get('issued', 0)
        modst['issued'] = b + 1
        k = b % 4
        S.op('pool', lambda h, b=b, k=k: h.dma_start(out=wa2[k][:], in_=wada_v[:, :, b * 256:(b + 1) * 256]),
             writes=[('ada', k)], dkey=('ada', k))
        S.op('sp', lambda h, b=b, k=k: h.dma_start(out=brow[k][:], in_=bada_d[0:1, b * 256:(b + 1) * 256]),
             writes=[('brow', k)], dkey=('brow', k))

    def mod_blocks(n):
        for _ in range(n):
            b = modst['next']
            if b >= 36:
                break
            modst['next'] = b + 1
            k = b % 2
            ka = b % 4
            for b2_ in range(b, b + 4):
                mod_issue(b2_)
            rb = 6 + k
            for kc in range(8):
                S.op('pe', lambda h, kc=kc, ka=ka, rb=rb: h.matmul(
                    banks[rb][0:1, 0:256], lhsT=silc[:, kc:kc + 1], rhs=wa2[ka][:, kc, :],
                    start=(kc == 0), stop=(kc == 7)), reads=['silc', ('ada', ka)], writes=[bk(rb)])
            S.op('dve', lambda h, k=k, ka=ka, rb=rb: h.tensor_tensor(out=modrow[k][:], in0=banks[rb][0:1, 0:256],
                                                                     in1=brow[ka][:], op=ALU.add),
                 reads=[bk(rb), ('brow', ka)], writes=[('modrow', k)])
            if modst['pend'] is not None:
                mod_cols(modst['pend'])
            modst['pend'] = b
        if modst['next'] >= 36 and modst['pend'] is not None:
            mod_cols(modst['pend'])
            modst['pend'] = None
        for b2_ in range(modst['next'], modst['next'] + 3):
            mod_issue(b2_)

    def mod_flush():
        if modst['pend'] is not None:
            mod_cols(modst['pend'])
            modst['pend'] = None
    MODUP = dict(done=False)

    def mod_upfront():
        if not MODUP['done']:
            MODUP['done'] = True
            mod_blocks(8)
            mod_flush()
    SH1 = modT[:, 0:8]
    SH2 = modT[:, 24:32]
    SH3 = modT[:, 48:56]
    GT2 = modT[:, 40:48]
    GFIN = gvec[:, 24:32]
    PSC = gvec[:, 32:36]

    def hres(tt, c):
        return ('hT', tt, c)

    def load_x(st, after_t0=None):
        Tb = (st - 2) * ST
        for i in range(8):
            k = i % 2
            S.op('sp', lambda h, i=i, k=k: h.dma_start(out=xs[k][:], in_=x4[Tb + 128 * i:Tb + 128 * (i + 1), :]),
                 writes=[('xs', k)], dkey=('xs', k))
            tt = i // 4
            for half in range(2):
                b = 6 + half
                for cc in range(4):
                    c = 4 * half + cc
                    S.op('pe', lambda h, k=k, c=c, cc=cc, b=b: h.transpose(
                        out=banks[b][:, cc * 128:(cc + 1) * 128], in_=xs[k][:, c * 128:(c + 1) * 128],
                        identity=identF[:]), reads=[('xs', k), 'identF'], writes=[bk(b)])
                dst = hT[:, 4 * half:4 * half + 4, i * 128:(i + 1) * 128]
                src = banks[b][:, :].rearrange("p (a b) -> p a b", a=4)
                wr = [hres(tt, 4 * half + cc) for cc in range(4)]
                if half == 0:
                    S.op('act', lambda h, dst=dst, src=src: h.activation(out=dst, in_=src, func=AF.Copy),
                         reads=[bk(b)], writes=wr)
                else:
                    S.op('dve', lambda h, dst=dst, src=src: h.tensor_copy(out=dst, in_=src),
                         reads=[bk(b)], writes=wr)
            if i == 3 and after_t0 is not None:
                after_t0()

    def norm_stats(tt):
        cols = slice(tt * 512, (tt + 1) * 512)
        for c in range(8):
            S.op('act', lambda h, c=c: h.activation(out=sq[c % 2][:], in_=hT[:, c, cols], func=AF.Square),
                 reads=[hres(tt, c)], writes=[('sq', c % 2)])
            S.op('pe', lambda h, c=c: h.matmul(banks[6][:, :], lhsT=onesB[:], rhs=sq[c % 2][:],
                                               start=(c == 0), stop=(c == 7)),
                 reads=[('sq', c % 2), 'onesB'], writes=[bk(6)])
        S.op('act', lambda h: h.activation(out=std[:], in_=banks[6][:, :], func=AF.Ln, scale=1.0 / D, bias=epsv[:, 0:1]),
             reads=[bk(6), 'epsv'], writes=['rstd'])
        S.op('act', lambda h: h.activation(out=rstd[:], in_=std[:], func=AF.Exp, scale=-0.5),
             reads=['rstd'], writes=['rstd'])

    def norm_to_nT(tt, Gv, SHv, gres, shres):
        norm_stats(tt)
        cols = slice(tt * 512, (tt + 1) * 512)
        for c in range(8):
            S.op('dve', lambda h, c=c: h.tensor_tensor(out=tmp[c % 2][:], in0=hT[:, c, cols], in1=rstd[:],
                                                       op=ALU.mult),
                 reads=[hres(tt, c), 'rstd'], writes=[('tmp', c % 2)])
            S.op('act', lambda h, c=c: h.activation(out=nT[:, c, cols], in_=tmp[c % 2][:], func=AF.Identity,
                                                    scale=Gv[:, c:c + 1], bias=SHv[:, c:c + 1]),
                 reads=[('tmp', c % 2), gres, shres], writes=[('nT', tt)])

    def ffn(which, GTv, gtres, after_t0=None, iter_hook=None):
        iters = [(gi, tt) for gi in range(NGRP - 2) for tt in range(2)]
        iters += [(NGRP - 2, 0), (NGRP - 1, 0), (NGRP - 2, 1), (NGRP - 1, 1)]
        prefetch_pts = set((gi, 0) for gi in range(NGRP - 1)) | {(NGRP - 2, 1), (NGRP - 1, 1)}
        slots = {}
        prev = [None]

        def down_piece(p):
            if prev[0] is None:
                return
            s, tt, cols, aset = prev[0]
            for c in (2 * p, 2 * p + 1):
                b = 4 + (c % 2)
                for j in range(2):
                    S.op('pe', lambda h, c=c, j=j, b=b, s=s, aset=aset: h.matmul(
                        banks[b][:, :], lhsT=wdn[s][:, j, c * 128:(c + 1) * 128], rhs=actT[aset][j][:],
                        start=(j == 0), stop=(j == 1)),
                        reads=[('W', s, 'd'), ('actT', aset, j)], writes=[bk(b)])
                S.op('dve', lambda h, c=c, b=b, cols=cols: h.scalar_tensor_tensor(
                    out=hT[:, c, cols], in0=banks[b][:, :], scalar=GTv[:, c:c + 1], in1=hT[:, c, cols],
                    op0=ALU.mult, op1=ALU.add),
                    reads=[bk(b), gtres, hres(tt, c)], writes=[hres(tt, c)])

        for it, (gi, tt) in enumerate(iters):
            if iter_hook is not None:
                iter_hook(it)
            if gi not in slots:
                slots[gi] = w_acquire(('ffn', which, gi))
            s = slots[gi]
            cols = slice(tt * 512, (tt + 1) * 512)
            aset = it % 2
            for j in range(2):
                for kc in range(8):
                    S.op('pe', lambda h, kc=kc, j=j, s=s, cols=cols: h.matmul(
                        banks[j][:, :], lhsT=wgu[s][:, kc, j * 128:(j + 1) * 128], rhs=nT[:, kc, cols],
                        start=(kc == 0), stop=(kc == 7)),
                        reads=[('W', s, 'g'), ('nT', tt)], writes=[bk(j)])
                down_piece(2 * j)
                for kc in range(8):
                    S.op('pe', lambda h, kc=kc, j=j, s=s, cols=cols: h.matmul(
                        banks[2 + j][:, :], lhsT=wgu[s][:, kc, 256 + j * 128:256 + (j + 1) * 128],
                        rhs=nT[:, kc, cols], start=(kc == 0), stop=(kc == 7)),
                        reads=[('W', s, 'u'), ('nT', tt)], writes=[bk(2 + j)])
                S.op('act', lambda h, j=j: h.activation(out=sg[j][:], in_=banks[j][:, :], func=AF.Silu),
                     reads=[bk(j)], writes=[('sg', j)])
                S.op('dve', lambda h, j=j, aset=aset: h.tensor_tensor(
                    out=actT[aset][j][:], in0=sg[j][:], in1=banks[2 + j][:, :], op=ALU.mult),
                    reads=[('sg', j), bk(2 + j)], writes=[('actT', aset, j)])
                down_piece(2 * j + 1)
            prev[0] = (s, tt, cols, aset)
            if (gi, tt) in prefetch_pts:
                w_prefetch()
            if (gi, tt) == (NGRP - 2, 1) and after_t0 is not None:
                after_t0()
        for p in range(4):
            down_piece(p)
        prev[0] = None
        w_prefetch()

    def proj_chunk(s, gq, tt, ncols=512, c0=0):
        b = proj_chunk.rot % 2 + 6
        proj_chunk.rot += 1
        cols = slice(tt * 512 + c0, tt * 512 + c0 + ncols)
        for kc in range(8):
            S.op('pe', lambda h, kc=kc, b=b: h.matmul(
                banks[b][:, 0:ncols], lhsT=wgu[s][:, kc, gq * 128:(gq + 1) * 128], rhs=nT[:, kc, cols],
                start=(kc == 0), stop=(kc == 7)), reads=WALL(s) + [('nT', tt)], writes=[bk(b)])
        return b
    proj_chunk.rot = 0
    evq = [0]

    def evac_copy(dst, src, reads, writes):
        evq[0] += 1
        if evq[0] % 2:
            S.op('act', lambda h: h.activation(out=dst, in_=src, func=AF.Copy), reads=reads, writes=writes)
        else:
            S.op('dve', lambda h: h.tensor_copy(out=dst, in_=src), reads=reads, writes=writes)

    def proj_kv(st, q):
        s = w_acquire(('win', q))
        w_prefetch()
        store = KT if q == 2 else VT
        sres = 'KT' if q == 2 else 'VT'
        for tt in range(2):
            for gq in range(4):
                b = proj_chunk(s, gq, tt)
                T0 = st * ST + tt * 512
                evac_copy(store[:, gq, T0:T0 + 512], banks[b][:, :], [bk(b)], [(sres, st, gq)])

    def proj_q(st):
        s = w_acquire(('win', 1))
        w_prefetch()
        for tt in range(2):
            cols = slice(tt * 512, (tt + 1) * 512)
            for gq in range(4):
                b = proj_chunk(s, gq, tt)
                S.op('act', lambda h, gq=gq, b=b, cols=cols: h.activation(
                    out=QTm[0:64, 0, gq, cols], in_=banks[b][0:64, :], func=AF.Copy),
                    reads=[bk(b)], writes=[('QT', gq)])
                S.op('dve', lambda h, gq=gq, b=b, cols=cols: h.tensor_copy(
                    out=QTm[64:128, 1, gq, cols], in_=banks[b][64:128, :]),
                    reads=[bk(b)], writes=[('QT', gq)])

    def proj_u_send(st):
        s = w_acquire(('win', 0))
        w_prefetch()
        for g in range(4):
            b = proj_chunk(s, g, 1, ncols=16, c0=496)
            S.op('dve', lambda h, g=g, b=b: h.tensor_copy(out=tmp[0][:, g * 16:(g + 1) * 16], in_=banks[b][:, 0:16]),
                 reads=[bk(b)], writes=[('tmp', 0)])
        if st == 2:
            S.op('sp', lambda h: h.dma_start(out=stashU[:, :], in_=tmp[0][:, 0:64]), reads=[('tmp', 0)],
                 writes=['stashU'], dkey='stashU')
            return
        S.op('sp', lambda h: h.dma_start(out=sendU[:, :], in_=tmp[0][:, 0:64]), reads=[('tmp', 0)],
             writes=['sendU'], dkey='sendU')
        S.op('pool', lambda h: h.collective_compute("AllGather", ALU.bypass, replica_groups=RG, dma_qos="P2",
                                                     ins=[sendU.ap().opt()], outs=[gathU.ap().opt()]),
             reads=['sendU'], writes=['gathU'], dkey='ccU', inc=1)

    def recv_halo(slist):
        for s_ in slist:
            for (gt, T, nm) in ((gathK[s_], KT, 'KT'), (gathV[s_], VT, 'VT')):
                for g_ in range(4):
                    S.op('pool', lambda h, gt=gt, T=T, s_=s_, g_=g_: h.indirect_dma_start(
                        out=T[:, g_, s_ * ST:(s_ + 1) * ST], out_offset=None,
                        in_=gt.ap().rearrange("r (g t) -> (r g) t", g=4),
                        in_offset=bass.IndirectOffsetOnAxis(ap=idx_sb[:, g_:g_ + 1], axis=0)),
                        reads=[('gath', nm, s_), 'idx'], writes=[(nm, s_, g_)],
                        dkey=('recv', nm, s_, g_))

    def recv_u():
        S.op('pool', lambda h: h.indirect_dma_start(
            out=carry[:].rearrange("p a b -> p (a b)"), out_offset=None, in_=gathU[:, :],
            in_offset=bass.IndirectOffsetOnAxis(ap=idx_sb[:, 4:5], axis=0)),
            reads=['gathU', 'idx'], writes=[('carry', g_) for g_ in range(4)], dkey='recvU')
        for g in range(4):
            S.op('dve', lambda h, g=g: h.tensor_scalar(out=carry[:, g, :], in0=carry[:, g, :], scalar1=flag[:, 0:1],
                                                        scalar2=None, op0=ALU.mult),
                 reads=[('carry', g), 'flag'], writes=[('carry', g)])

    def carry_from_stash():
        S.op('sp', lambda h: h.dma_start(out=carry[:].rearrange("p a b -> p (a b)"), in_=stashU[:, :]),
             reads=['stashU'], writes=[('carry', g_) for g_ in range(4)], dkey='unstash')

    def send_kv(st, which=None):
        s_ = st - 2
        T0 = st * ST
        for (T, snd, gt, nm) in ((KT, sendK[s_], gathK[s_], 'KT'), (VT, sendV[s_], gathV[s_], 'VT')):
            if which is not None and nm != which:
                continue
            S.op('sp', lambda h, T=T, snd=snd: h.dma_start(
                out=snd.ap().rearrange("p (g t) -> p g t", g=4), in_=T[:, :, T0:T0 + ST]),
                reads=[(nm, st, g_) for g_ in range(4)], writes=[('send', nm, s_)], dkey=('send', nm, s_))
            S.op('pool', lambda h, snd=snd, gt=gt: h.collective_compute(
                "AllGather", ALU.bypass, replica_groups=RG, dma_qos="P2", ins=[snd.ap().opt()], outs=[gt.ap().opt()]),
                reads=[('send', nm, s_)], writes=[('gath', nm, s_)], dkey=('cc', nm, s_), inc=1)

    def spill_h(st, tiles=(0, 1)):
        s_ = st - 2
        for tt in tiles:
            S.op('sp', lambda h, tt=tt: h.dma_start(
                out=h1buf[s_].ap().rearrange("p (c t) -> p c t", c=8)[:, :, tt * 512:(tt + 1) * 512],
                in_=hT[:, :, tt * 512:(tt + 1) * 512]),
                reads=[hres(tt, c) for c in range(8)], writes=[('h1buf', s_, tt)], dkey=('spill', s_, tt))

    def spill_n(st, tiles=(0, 1)):
        s_ = st - 2
        for tt in tiles:
            S.op('sp', lambda h, tt=tt: h.dma_start(
                out=n2buf[s_].ap().rearrange("p (c t) -> p c t", c=8)[:, :, tt * 512:(tt + 1) * 512],
                in_=nT[:, :, tt * 512:(tt + 1) * 512]),
                reads=[('nT', tt)], writes=[('n2buf', s_, tt)], dkey=('spilln', s_, tt))

    def reload_n(st, tiles=(0, 1)):
        s_ = st - 2
        for tt in tiles:
            S.op('sp', lambda h, tt=tt: h.dma_start(
                out=nT[:, :, tt * 512:(tt + 1) * 512],
                in_=n2buf[s_].ap().rearrange("p (c t) -> p c t", c=8)[:, :, tt * 512:(tt + 1) * 512]),
                reads=[('n2buf', s_, tt)], writes=[('nT', tt)], dkey=('reloadn', tt))

    def reload_h(st, tiles=(0, 1)):
        s_ = st - 2
        for tt in tiles:
            S.op('sp', lambda h, tt=tt: h.dma_start(
                out=hT[:, :, tt * 512:(tt + 1) * 512],
                in_=h1buf[s_].ap().rearrange("p (c t) -> p c t", c=8)[:, :, tt * 512:(tt + 1) * 512]),
                reads=[('h1buf', s_, tt)], writes=[hres(tt, c) for c in range(8)], dkey=('reload', tt))

    def proj_u_pool(st):
        s = w_acquire(('win', 0))
        w_prefetch()
        sq_ = w_acquire(('win', 1))
        it = 0
        pend = [None]
        for tt in range(2):
            cols = slice(tt * 512, (tt + 1) * 512)
            for g in range(4):
                w = (2, 4, 8, 16)[g]
                b = proj_chunk(s, g, tt)
                k = it % 2
                it += 1
                S.op('act', lambda h, k=k, b=b: h.activation(out=ub[k][:, 16:528], in_=banks[b][:, :], func=AF.Copy),
                     reads=[bk(b)], writes=[('ub', k)])
                S.op('dve', lambda h, k=k, g=g: h.tensor_copy(out=ub[k][:, 0:16], in_=carry[:, g, :]),
                     reads=[('carry', g)], writes=[('ub', k)])
                S.op('dve', lambda h, k=k, g=g: h.tensor_copy(out=carry[:, g, :], in_=ub[k][:, 512:528]),
                     reads=[('ub', k)], writes=[('carry', g)])
                curt, curres = ub[k], ('ub', k)
                lo = 0
                pi = 0
                step = 1
                while step < w:
                    lo += step
                    nxt, nres = pw[pi % 2], ('pw', pi % 2)
                    S.op('dve', lambda h, curt=curt, nxt=nxt, lo=lo, step=step: h.tensor_tensor(
                        out=nxt[:, lo:528], in0=curt[:, lo:528], in1=curt[:, lo - step:528 - step], op=ALU.add),
                        reads=[curres], writes=[nres])
                    curt, curres = nxt, nres
                    pi += 1
                    step *= 2
                if st == 2 and tt == 0:
                    S.op('dve', lambda h, curt=curt, g=g: h.tensor_tensor(
                        out=curt[:, 16:32], in0=curt[:, 16:32], in1=pcfix[:, g, :], op=ALU.mult),
                        reads=[curres, 'pcfix'], writes=[curres])
                S.op('dve', lambda h, curt=curt, k=k, w=w: h.scalar_tensor_tensor(
                    out=pooled[k][:], in0=curt[:, 16:528], scalar=1.0 / w, in1=ub[k][:, 16:528],
                    op0=ALU.mult, op1=ALU.subtract), reads=[curres, ('ub', k)], writes=[('pooled', k)])
                bq = proj_chunk(sq_, g, tt)
                S.op('act', lambda h, g=g, bq=bq, cols=cols: h.activation(
                    out=QTm[0:64, 0, g, cols], in_=banks[bq][0:64, :], func=AF.Copy),
                    reads=[bk(bq)], writes=[('QT', g)])
                S.op('act', lambda h, g=g, bq=bq, cols=cols: h.activation(
                    out=QTm[64:128, 1, g, cols], in_=banks[bq][64:128, :], func=AF.Copy),
                    reads=[bk(bq)], writes=[('QT', g)])

                def pooled_mm(g=g, k=k, cols=cols):
                    b2 = 4 + k
                    S.op('pe', lambda h: h.matmul(banks[b2][:, :], lhsT=wpool[:, g, :], rhs=pooled[k][:],
                                                  start=True, stop=True),
                         reads=['wpool', ('pooled', k)], writes=[bk(b2)])
                    S.op('act', lambda h: h.activation(
                        out=yT[:, g, cols], in_=banks[b2][:, :], func=AF.Identity, scale=PSC[:, g:g + 1]),
                        reads=[bk(b2), 'gvec'], writes=[('yT', g)])
                if pend[0] is not None:
                    pend[0]()
                pend[0] = pooled_mm
        pend[0]()
        w_prefetch()

    VtAll4 = VtAll[:, :, :].rearrange("p s (a b) -> p s a b", a=4)
    DMv = DM[:, :, :].rearrange("p h (t c) -> p h t c", t=2)

    def attention(st):
        qbase = st * ST
        rot = dict(A=0, B=0, C=0)

        def attn_pair(g):
            blocks = []
            for b in range(8):
                blocks.append((1, qbase + 128 * b, 128, qbase + 128 * b - 128, qbase + 128 * b, 0))
            for r in range(4):
                for b in range(2):
                    q0 = qbase + r + 512 * b
                    blocks.append((4, q0, 128, q0 - 512, q0, 0))
            for r in range(16):
                blocks.append((16, qbase + r, 64, r, 2048 + r, 0 if st == 2 else 64))
            pending = []
            for bi, (d, qT0, nq, pT0, dT0, qoff) in enumerate(blocks):
                nh = sum(1 for i in range(128) if pT0 + d * i < 2048)
                var = {128: 'A', 0: 'C'}[nh]
                cpair = rot['C'] % 3
                rot['C'] += 1
                vds = 4 + 2 * cpair
                if var == 'C':
                    vps, valid, vres = 3 + 2 * cpair, None, None
                else:
                    vps, valid, vres = rot['A'] % 3, flag, 'flag'
                    rot['A'] += 1
                sts = sorted(set([(pT0 + d * i) // ST for i in (0, 127)] + [(dT0 + d * i) // ST for i in (0, 127)]))
                sts = list(range(sts[0], sts[-1] + 1))
                if st == 3:
                    sts = [x_ for x_ in sts if x_ != 0]
                vtreads = [('VT', st2, g) for st2 in sts] + ['identB']
                kreads = [('KT', st2, g) for st2 in sts] + [('QT', g)]
                tb = 6 + (bi % 2)
                sb = bi % 4
                pi = bi % 4
                pt, ptres = PT[pi], ('PT', pi)
                pkeys = slice(pT0, pT0 + 127 * d + 1, d)
                dsl = slice(dT0, dT0 + 127 * d + 1, d)

                def dk(T, g=g, dsl=dsl):
                    return T[:, g, dsl]
                ql0 = qT0 - qbase
                qcols = slice(ql0, ql0 + (nq - 1) * d + 1, d)
                idx0 = 2 * g + 4 - {1: 0, 4: 2, 16: 4}[d]
                S.op('pe', lambda h, tb=tb, pkeys=pkeys, g=g: h.transpose(
                    out=banksB[tb][:, 0:128], in_=VT[:, g, pkeys], identity=identB[:]),
                    reads=vtreads, writes=[bk(tb)])
                S.op('pe', lambda h, tb=tb, dk=dk: h.transpose(
                    out=banksB[tb][:, 128:256], in_=dk(VT), identity=identB[:]),
                    reads=vtreads, writes=[bk(tb)])
                if var == 'C':
                    S.op('act', lambda h, tb=tb, vps=vps: h.activation(
                        out=VtAll4[:, vps:vps + 2, 0:4:3, :],
                        in_=banksB[tb][:, 0:256].rearrange("p (s a b) -> p s a b", s=2, a=2),
                        func=AF.Copy), reads=[bk(tb)], writes=[('Vt', vps), ('Vt', vds)])
                else:
                    S.op('act', lambda h, tb=tb, vps=vps, valid=valid: h.activation(
                        out=VtAll4[:, vps, 0:4:3, :], in_=banksB[tb][:, 0:128].rearrange("p (a b) -> p a b", a=2),
                        func=AF.Identity, scale=valid[:, 0:1]), reads=[bk(tb), vres], writes=[('Vt', vps)])
                    S.op('act', lambda h, tb=tb, vds=vds: h.activation(
                        out=VtAll4[:, vds, 0:4:3, :], in_=banksB[tb][:, 128:256].rearrange("p (a b) -> p a b", a=2),
                        func=AF.Copy), reads=[bk(tb)], writes=[('Vt', vds)])
                S.op('pe', lambda h, sb=sb, pkeys=pkeys, qcols=qcols, nq=nq, g=g: h.matmul(
                    banks[sb][:, 0:2 * nq], lhsT=KT[:, g, pkeys], rhs=QTm[:, :, g, qcols],
                    start=True, stop=True), reads=kreads, writes=[bk(sb)])
                S.op('pe', lambda h, sb=sb, dk=dk, qcols=qcols, nq=nq, g=g: h.matmul(
                    banks[sb][:, 2 * nq:4 * nq], lhsT=dk(KT), rhs=QTm[:, :, g, qcols],
                    start=True, stop=True), reads=kreads, writes=[bk(sb)])
                S.op('act', lambda h, sb=sb, pt=pt, nq=nq: h.activation(
                    out=pt[:, 0:4 * nq], in_=banks[sb][:, 0:4 * nq], func=AF.Exp, scale=0.125),
                    reads=[bk(sb)], writes=[ptres])
                S.op('dve', lambda h, pt=pt, nq=nq, idx0=idx0, qoff=qoff: h.tensor_tensor(
                    out=pt[:, 0:4 * nq].rearrange("p (t h c) -> p t h c", t=2, h=2),
                    in0=pt[:, 0:4 * nq].rearrange("p (t h c) -> p t h c", t=2, h=2),
                    in1=DMv[:, idx0:idx0 + 2, :, qoff:qoff + nq].rearrange("p h t c -> p t h c"), op=ALU.mult),
                    reads=[ptres, ('DM', idx0), ('DM', idx0 + 1)], writes=[ptres])

                def pv(pt=pt, ptres=ptres, vps=vps, vds=vds, nq=nq, qcols=qcols, first=(d == 1)):
                    for hh in range(2):
                        ob = 4 + hh
                        S.op('pe', lambda h, hh=hh, ob=ob: h.matmul(
                            banks[ob][:, 0:nq], lhsT=VtAll[:, vps, hh * 128:(hh + 1) * 128],
                            rhs=pt[:, hh * nq:(hh + 1) * nq], start=True, stop=False),
                            reads=[('Vt', vps), ptres], writes=[bk(ob)])
                        S.op('pe', lambda h, hh=hh, ob=ob: h.matmul(
                            banks[ob][:, 0:nq], lhsT=VtAll[:, vds, hh * 128:(hh + 1) * 128],
                            rhs=pt[:, 2 * nq + hh * nq:2 * nq + (hh + 1) * nq], start=False, stop=True),
                            reads=[('Vt', vds), ptres], writes=[bk(ob)])
                        if first:
                            S.op('dve', lambda h, hh=hh, ob=ob: h.tensor_copy(
                                out=Oacc[:, hh, qcols], in_=banks[ob][:, 0:nq]),
                                reads=[bk(ob)], writes=[('Oacc', hh)])
                        else:
                            S.op('dve', lambda h, hh=hh, ob=ob: h.tensor_tensor(
                                out=Oacc[:, hh, qcols], in0=banks[ob][:, 0:nq], in1=Oacc[:, hh, qcols], op=ALU.add),
                                reads=[bk(ob), ('Oacc', hh)], writes=[('Oacc', hh)])
                pending.append(pv)
                if len(pending) > 2:
                    pending.pop(0)()
            while pending:
                pending.pop(0)()
            for hf in range(2):
                cs = slice(hf * 512, (hf + 1) * 512)
                S.op('act', lambda h, cs=cs: h.activation(out=rec[0:64, :], in_=Oacc[64:128, 0, cs], func=AF.Ln),
                     reads=[('Oacc', 0)], writes=['rec'])
                S.op('act', lambda h, cs=cs: h.activation(out=rec[64:128, :], in_=Oacc[0:64, 1, cs], func=AF.Ln),
                     reads=[('Oacc', 1)], writes=['rec'])
                S.op('act', lambda h: h.activation(out=rec[:, :], in_=rec[:, :], func=AF.Exp, scale=-1.0),
                     reads=['rec'], writes=['rec'])
                S.op('dve', lambda h, g=g, cs=cs: h.tensor_tensor(
                    out=yT[0:64, 4 + g, cs], in0=Oacc[0:64, 0, cs], in1=rec[0:64, :], op=ALU.mult),
                    reads=[('Oacc', 0), 'rec'], writes=[('yT', 4 + g)])
                S.op('dve', lambda h, g=g, cs=cs: h.tensor_tensor(
                    out=yT[64:128, 4 + g, cs], in0=Oacc[64:128, 1, cs], in1=rec[64:128, :], op=ALU.mult),
                    reads=[('Oacc', 1), 'rec'], writes=[('yT', 4 + g)])
        for g in range(4):
            attn_pair(g)

    def wout_phase(after_t0=None):
        s0 = w_acquire(('wout', 0))
        w_prefetch()
        s1 = w_acquire(('wout', 1))
        sl = [s0, s1]
        for tt in range(2):
            cols = slice(tt * 512, (tt + 1) * 512)
            for c in range(8):
                b = 4 + (c % 2)
                for kc in range(8):
                    S.op('pe', lambda h, kc=kc, c=c, b=b, cols=cols: h.matmul(
                        banks[b][:, :], lhsT=wov[sl[kc // 4]][:, kc % 4, c * 128:(c + 1) * 128], rhs=yT[:, kc, cols],
                        start=(kc == 0), stop=(kc == 7)),
                        reads=WALL(sl[kc // 4]) + [('yT', kc)], writes=[bk(b)])
                S.op('dve', lambda h, c=c, b=b, cols=cols: h.scalar_tensor_tensor(
                    out=hT[:, c, cols], in0=banks[b][:, :], scalar=GT2[:, c:c + 1], in1=hT[:, c, cols],
                    op0=ALU.mult, op1=ALU.add), reads=[bk(b), mres(5), hres(tt, c)], writes=[hres(tt, c)])
            if tt == 0 and after_t0 is not None:
                after_t0()
        w_prefetch()

    def final_tile(st, tt):
        if True:
            norm_stats(tt)
            for i in range(4):
                c0 = tt * 512 + i * 128
                k = (tt * 4 + i) % 2
                for c in range(8):
                    S.op('dve', lambda h, c=c, c0=c0, i=i: h.scalar_tensor_tensor(
                        out=ot[:, c, :], in0=hT[:, c, c0:c0 + 128], scalar=GFIN[:, c:c + 1],
                        in1=rstd[:, i * 128:(i + 1) * 128], op0=ALU.mult, op1=ALU.mult),
                        reads=[hres(tt, c), 'gvec', 'rstd'], writes=[('ot', c)])
                for half in range(2):
                    b = 6 + half
                    for cc in range(4):
                        c = 4 * half + cc
                        S.op('pe', lambda h, c=c, cc=cc, b=b: h.transpose(
                            out=banks[b][:, cc * 128:(cc + 1) * 128], in_=ot[:, c, :], identity=identF[:]),
                            reads=[('ot', c), 'identF'], writes=[bk(b)])
                    dst = ostage[k][:, half * 512:(half + 1) * 512]
                    if half == 0:
                        S.op('act', lambda h, dst=dst, b=b: h.activation(out=dst, in_=banks[b][:, :], func=AF.Copy),
                             reads=[bk(b)], writes=[('ostage', k)])
                    else:
                        S.op('dve', lambda h, dst=dst, b=b: h.tensor_copy(out=dst, in_=banks[b][:, :]),
                             reads=[bk(b)], writes=[('ostage', k)])
                r0 = (st - 2) * ST + c0
                S.op('sp', lambda h, k=k, r0=r0: h.dma_start(out=out_d[r0:r0 + 128, :], in_=ostage[k][:]),
                     reads=[('ostage', k)], dkey=('ostage', k))

    for st in (3, 2):
        load_x(st, after_t0=lambda: (mod_upfront(), norm_to_nT(0, G1, SH1, 'G1', mres(0))))
        if st == 2:
            send_kv(3)
        norm_to_nT(1, G1, SH1, 'G1', mres(0))
        ffn(1, GT1, 'GT1',
            after_t0=lambda st=st: (norm_to_nT(0, G2, SH2, 'G2', mres(3)), spill_h(st, (0,)),
                                    spill_n(st, (0,)) if st == 2 else None,
                                    reload_h(3, (0,)) if st == 2 else None),
            iter_hook=(lambda it: (mod_blocks(4 if it == 0 else 2),
                                   kv_zero_part(it - 14) if 14 <= it < 18 else None)) if st == 3 else None)
        norm_to_nT(1, G2, SH2, 'G2', mres(3))
        spill_h(st, (1,))
        if st == 2:
            spill_n(st, (1,))
        if st == 2:
            reload_h(3, (1,))
        proj_kv(st, 2)
        proj_kv(st, 3)
        proj_u_send(st)
    w_prefetch(ahead=2)
    send_kv(2)
    norm_to_nT(0, G2, SH2, 'G2', mres(3))
    norm_to_nT(1, G2, SH2, 'G2', mres(3))
    for st in (3, 2):
        if st == 2:
            recv_u()
        else:
            carry_from_stash()
        proj_u_pool(st)
        recv_halo((1,) if st == 3 else (0,))
        attention(st)
        wout_phase(after_t0=lambda: norm_to_nT(0, G3, SH3, 'G3', mres(6)))
        norm_to_nT(1, G3, SH3, 'G3', mres(6))
        if st == 3:
            ffn(2, GT3, 'GT3', after_t0=lambda: (final_tile(3, 0), reload_h(2, (0,)), reload_n(2, (0,))))
            final_tile(3, 1)
            reload_h(2, (1,))
            reload_n(2, (1,))
        else:
            ffn(2, GT3, 'GT3', after_t0=lambda: final_tile(2, 0))
            final_tile(2, 1)
    assert W['acq'] == len(wseq), (W, len(wseq))
    S.emit(nc, final_wait_keys=[('ostage', 0), ('ostage', 1)])
    return nc


def _host_consts():
    ident = np.eye(128, dtype=np.float32)
    ki = np.arange(128)[:, None].astype(np.float64)
    qi = np.arange(128)[None, :].astype(np.float64)
    dm = np.zeros((128, 12, 256), dtype=np.float32)
    for idx in range(12):
        a = 2.0 ** (3 - idx)
        e = idx - 8
        prev = np.where(ki >= qi, np.exp(-a * np.clip(qi + 128 - ki, 0, 256)), 0.0)
        diag = np.where(ki <= qi, np.exp(-a * np.clip(qi - ki, 0, 256)), 0.0)
        dm[:, e + 8, 0:128] = prev
        dm[:, e + 8, 128:256] = diag
    return ident, dm.reshape(128, 12 * 256)


def _colT(v, n):
    return np.ascontiguousarray(np.asarray(v, dtype=np.float32).reshape(n, 128).T)


_NC_CACHE = {}


def kernel(x, c, w_ada, b_ada, g_ffn1, w1_gate, w1_up, w1_down, g_mix, w_in, w_pool,
           pool_scale, w_out, g_ffn2, w2_gate, w2_up, w2_down, g_final):
    f = lambda a: np.ascontiguousarray(np.asarray(a, dtype=np.float32))
    x = f(x)
    c = f(c)
    ident, dm = _host_consts()
    gvec = np.concatenate([_colT(f(g_ffn1)[0], 8), _colT(f(g_mix)[0], 8), _colT(f(g_ffn2)[0], 8),
                           _colT(f(g_final), 8), _colT(f(pool_scale)[0], 4)], axis=1)
    shared = {
        "w_ada": f(w_ada)[0], "bada": f(b_ada)[0].reshape(1, 9 * D), "gvec": np.ascontiguousarray(gvec),
        "w1_gate": f(w1_gate)[0], "w1_up": f(w1_up)[0], "w1_down": f(w1_down)[0],
        "w2_gate": f(w2_gate)[0], "w2_up": f(w2_up)[0], "w2_down": f(w2_down)[0],
        "w_in": f(w_in)[0], "w_pool": f(w_pool)[0], "w_out": f(w_out)[0],
        "ident": ident, "dmask": dm,
    }
    in_maps = []
    for core in range(NCORES):
        b, j = core // 4, core % 4
        x4 = np.ascontiguousarray(x[b, j * OWN:(j + 1) * OWN])
        base = max(j - 1, 0) * 128 + np.arange(128)
        idx = np.stack([4 * base + 0, 4 * base + 1, 4 * base + 2, 4 * base + 3, base], axis=1).astype(np.int32)
        flag = np.full((128, 1), 1.0 if j > 0 else 0.0, dtype=np.float32)
        pcfix = np.ones((128, 4, 16), dtype=np.float32)
        if j == 0:
            for g, w in enumerate((2, 4, 8, 16)):
                for t in range(16):
                    pcfix[:, g, t] = float(w) / float(min(t + 1, w))
        m = dict(shared)
        m.update({"x4": x4, "cT": _colT(c[b], 8), "flag": flag, "pcfix": pcfix.reshape(128, 64), "idx": idx})
        in_maps.append(m)
    if "nc" not in _NC_CACHE:
        _NC_CACHE["nc"] = build_program()
    nc = _NC_CACHE["nc"]
    res = run_bass_kernel_spmd(nc, in_maps, core_ids=list(range(NCORES)))
    out = np.zeros((2, 4 * OWN, D), dtype=np.float32)
    for core in range(NCORES):
        b, j = core // 4, core % 4
        out[b, j * OWN:(j + 1) * OWN] = res.results[core]["out"]
    return out
```
